# Optimizing a Trainium2 kernel written in Bass

```python
import jax, jax.numpy as jnp
from jax import lax
import numpy as np

D_MODEL = 2048
BATCH = 2
SEQ = 8192
DEPTH = 4

GRID_W = 64
CTX_LEN = 256
N_MIXERS = 3
N_A = (DEPTH + 2) // 3
N_B = (DEPTH + 1) // 3
N_C = DEPTH // 3
CHUNK = 128
GMLP_WIDTH = D_MODEL
GMLP_GROUP_DIM = 128
GMLP_GROUPS = GMLP_WIDTH // GMLP_GROUP_DIM
NA_HEAD_DIM = 128
NA_HEADS = D_MODEL // NA_HEAD_DIM
NA_MAX_ROWS = 8
NA_COLS = 16
CONV_WIDTH = 31
FFN_HIDDEN = (((8 * D_MODEL + 2) // 3 + 255) // 256) * 256
EPS = 1e-6
NEG_INF = -1e30

kernel_name = "hybrid_interleaved_dit_backbone"


def rmsnorm(x, g):
    xf = x.astype(jnp.float32)
    y = xf * lax.rsqrt(jnp.mean(xf * xf, axis=-1, keepdims=True) + EPS)
    return (y * g.astype(jnp.float32)).astype(x.dtype)


def layernorm(x, g, b):
    xf = x.astype(jnp.float32)
    mu = jnp.mean(xf, axis=-1, keepdims=True)
    xc = xf - mu
    var = jnp.mean(xc * xc, axis=-1, keepdims=True)
    return (xc * lax.rsqrt(var + EPS) * g.astype(jnp.float32) + b.astype(jnp.float32)).astype(x.dtype)


def modulate(h, shift, scale):
    return h * (1 + scale) + shift


def swiglu(h, w1, w3, w2):
    return (jax.nn.silu(h @ w1) * (h @ w3)) @ w2


def gmlp_chunk_mixer(h, w_in, ln_g, ln_b, w_s, b_s, w_out):
    bsz, L, _ = h.shape
    z = jax.nn.gelu(h @ w_in)
    u, v = jnp.split(z, 2, axis=-1)
    v = layernorm(v, ln_g, ln_b)
    v = v.reshape(bsz, L // CHUNK, CHUNK, GMLP_GROUPS, GMLP_GROUP_DIM)
    v = jnp.einsum('gpq,bnqgc->bnpgc', w_s, v) + b_s.T[:, :, None]
    v = v.reshape(bsz, L, GMLP_WIDTH)
    return (u * v) @ w_out


def _heads(t):
    return t.reshape(t.shape[0], t.shape[1], NA_HEADS, NA_HEAD_DIM).transpose(0, 2, 1, 3)


def neighbourhood_attention(h_lat, h_ctx, w_qkv, rpb, w_out, need_ctx_out):
    bsz, L, _ = h_lat.shape
    rows = L // GRID_W
    kr = min(NA_MAX_ROWS, rows)
    scale = NA_HEAD_DIM ** -0.5

    q, k, v = jnp.split(h_lat @ w_qkv, 3, axis=-1)
    qc, kc, vc = jnp.split(h_ctx @ w_qkv, 3, axis=-1)
    q, k, v = _heads(q), _heads(k), _heads(v)
    qc, kc, vc = _heads(qc), _heads(kc), _heads(vc)
    grid = (bsz, NA_HEADS, rows, GRID_W, NA_HEAD_DIM)
    qg, kg, vg = q.reshape(grid), k.reshape(grid), v.reshape(grid)

    cols = jnp.arange(GRID_W)
    c_start = jnp.clip(cols - NA_COLS // 2, 0, GRID_W - NA_COLS)
    col_in = (cols[None, :] >= c_start[:, None]) & (cols[None, :] < c_start[:, None] + NA_COLS)
    mask = jnp.broadcast_to(col_in[:, None, :], (GRID_W, kr, GRID_W)).reshape(GRID_W, kr * GRID_W)
    dc_idx = jnp.clip(cols[None, :] - cols[:, None] + NA_COLS - 1, 0, 2 * NA_COLS - 2)

    def row_block(r):
        r_start = jnp.clip(r - kr // 2, 0, rows - kr)
        q_r = lax.dynamic_index_in_dim(qg, r, axis=2, keepdims=False)
        k_r = lax.dynamic_slice_in_dim(kg, r_start, kr, axis=2).reshape(bsz, NA_HEADS, kr * GRID_W, NA_HEAD_DIM)
        v_r = lax.dynamic_slice_in_dim(vg, r_start, kr, axis=2).reshape(bsz, NA_HEADS, kr * GRID_W, NA_HEAD_DIM)
        dr_idx = r_start + jnp.arange(kr) - r + (NA_MAX_ROWS - 1)
        bias = rpb[:, dr_idx][:, :, dc_idx]
        bias = bias.transpose(0, 2, 1, 3).reshape(NA_HEADS, GRID_W, kr * GRID_W)
        s_lat = jnp.einsum('bhqd,bhkd->bhqk', q_r, k_r).astype(jnp.float32) * scale + bias.astype(jnp.float32)
        s_lat = jnp.where(mask, s_lat, NEG_INF)
        s_ctx = jnp.einsum('bhqd,bhkd->bhqk', q_r, kc).astype(jnp.float32) * scale
        p = jax.nn.softmax(jnp.concatenate([s_lat, s_ctx], axis=-1), axis=-1).astype(v.dtype)
        return (jnp.einsum('bhqk,bhkd->bhqd', p[..., :kr * GRID_W], v_r)
                + jnp.einsum('bhqk,bhkd->bhqd', p[..., kr * GRID_W:], vc))

    o = lax.map(row_block, jnp.arange(rows))
    o = o.transpose(1, 0, 3, 2, 4).reshape(bsz, L, D_MODEL)
    out_lat = o @ w_out
    if not need_ctx_out:
        return out_lat, None
    s_c = jnp.einsum('bhqd,bhkd->bhqk', qc, kc).astype(jnp.float32) * scale
    p_c = jax.nn.softmax(s_c, axis=-1).astype(vc.dtype)
    oc = jnp.einsum('bhqk,bhkd->bhqd', p_c, vc).transpose(0, 2, 1, 3).reshape(bsz, h_ctx.shape[1], D_MODEL)
    return out_lat, oc @ w_out


def conformer_conv(h, w_pw1, w_dw, b_dw, ln_g, ln_b, w_pw2):
    a, g = jnp.split(h @ w_pw1, 2, axis=-1)
    y = a * jax.nn.sigmoid(g)
    y = lax.conv_general_dilated(
        y, w_dw[:, None, :], window_strides=(1,),
        padding=[(CONV_WIDTH // 2, CONV_WIDTH // 2)],
        dimension_numbers=('NWC', 'WIO', 'NWC'),
        feature_group_count=D_MODEL) + b_dw
    y = jax.nn.silu(layernorm(y, ln_g, ln_b))
    return y @ w_pw2


def setup_inputs(seed: int = 0) -> dict:
    key = jax.random.key(seed)
    keys = iter(jax.random.split(key, 40))
    D, F, E = D_MODEL, FFN_HIDDEN, GMLP_WIDTH

    def nrm(shape, scale):
        return jax.random.normal(next(keys), shape, jnp.float32) * scale

    def gain(shape):
        return 1.0 + nrm(shape, 0.02)

    return {
        "x": nrm((BATCH, SEQ, D), 1.0),
        "c": nrm((BATCH, D), 1.0),
        "ctx": nrm((BATCH, CTX_LEN, D), 1.0),
        "c_ctx": nrm((D,), 1.0),
        "ada_w": nrm((DEPTH, D, 6 * D), 0.5 * D ** -0.5),
        "ada_b": nrm((DEPTH, 6 * D), 0.01),
        "g_mix": gain((DEPTH, D)),
        "g_ffn": gain((DEPTH, D)),
        "ffn_w1": nrm((DEPTH, D, F), D ** -0.5),
        "ffn_w3": nrm((DEPTH, D, F), D ** -0.5),
        "ffn_w2": nrm((DEPTH, F, D), F ** -0.5),
        "a_w_in": nrm((N_A, D, 2 * E), D ** -0.5),
        "a_ln_g": gain((N_A, E)),
        "a_ln_b": nrm((N_A, E), 0.01),
        "a_w_s": nrm((N_A, GMLP_GROUPS, CHUNK, CHUNK), CHUNK ** -0.5),
        "a_b_s": gain((N_A, GMLP_GROUPS, CHUNK)),
        "a_w_out": nrm((N_A, E, D), E ** -0.5),
        "b_w_qkv": nrm((N_B, D, 3 * D), D ** -0.5),
        "b_rpb": nrm((N_B, NA_HEADS, 2 * NA_MAX_ROWS - 1, 2 * NA_COLS - 1), 0.1),
        "b_w_out": nrm((N_B, D, D), D ** -0.5),
        "c_w_pw1": nrm((N_C, D, 2 * D), D ** -0.5),
        "c_w_dw": nrm((N_C, CONV_WIDTH, D), CONV_WIDTH ** -0.5),
        "c_b_dw": nrm((N_C, D), 0.01),
        "c_ln_g": gain((N_C, D)),
        "c_ln_b": nrm((N_C, D), 0.01),
        "c_w_pw2": nrm((N_C, D, D), D ** -0.5),
        "g_final": gain((D,)),
    }


def reference(x, c, ctx, c_ctx, ada_w, ada_b, g_mix, g_ffn, ffn_w1, ffn_w3, ffn_w2,
              a_w_in, a_ln_g, a_ln_b, a_w_s, a_b_s, a_w_out,
              b_w_qkv, b_rpb, b_w_out,
              c_w_pw1, c_w_dw, c_b_dw, c_ln_g, c_ln_b, c_w_pw2, g_final):
    h_lat, h_ctx = x, ctx
    s_lat = jax.nn.silu(c)
    s_ctx = jax.nn.silu(c_ctx)
    for i in range(DEPTH):
        last = i == DEPTH - 1
        mixer = i % N_MIXERS
        j = i // N_MIXERS
        mod_l = (s_lat @ ada_w[i] + ada_b[i])[:, None, :]
        sh1, sc1, gt1, sh2, sc2, gt2 = jnp.split(mod_l, 6, axis=-1)
        ctx_needed = (not last) or mixer == 1
        if ctx_needed:
            mod_c = s_ctx @ ada_w[i] + ada_b[i]
            csh1, csc1, cgt1, csh2, csc2, cgt2 = jnp.split(mod_c, 6, axis=-1)
            hc = modulate(rmsnorm(h_ctx, g_mix[i]), csh1, csc1)
        hl = modulate(rmsnorm(h_lat, g_mix[i]), sh1, sc1)

        if mixer == 0:
            ml = gmlp_chunk_mixer(hl, a_w_in[j], a_ln_g[j], a_ln_b[j], a_w_s[j], a_b_s[j], a_w_out[j])
            mc = (gmlp_chunk_mixer(hc, a_w_in[j], a_ln_g[j], a_ln_b[j], a_w_s[j], a_b_s[j], a_w_out[j])
                  if not last else None)
        elif mixer == 1:
            ml, mc = neighbourhood_attention(hl, hc, b_w_qkv[j], b_rpb[j], b_w_out[j], not last)
        else:
            ml = conformer_conv(hl, c_w_pw1[j], c_w_dw[j], c_b_dw[j], c_ln_g[j], c_ln_b[j], c_w_pw2[j])
            mc = (conformer_conv(hc, c_w_pw1[j], c_w_dw[j], c_b_dw[j], c_ln_g[j], c_ln_b[j], c_w_pw2[j])
                  if not last else None)

        h_lat = h_lat + gt1 * ml
        hl = modulate(rmsnorm(h_lat, g_ffn[i]), sh2, sc2)
        h_lat = h_lat + gt2 * swiglu(hl, ffn_w1[i], ffn_w3[i], ffn_w2[i])
        if not last:
            h_ctx = h_ctx + cgt1 * mc
            hc = modulate(rmsnorm(h_ctx, g_ffn[i]), csh2, csc2)
            h_ctx = h_ctx + cgt2 * swiglu(hc, ffn_w1[i], ffn_w3[i], ffn_w2[i])
    return rmsnorm(h_lat, g_final)
```

```python
import contextlib
import numpy as np
import concourse.bass as bass
import concourse.mybir as mybir
from concourse.bass_utils import run_bass_kernel_spmd

F32 = mybir.dt.float32
BF16 = mybir.dt.bfloat16
AF = mybir.ActivationFunctionType
ALU = mybir.AluOpType

D = 2048
FF = 5632
NLOC = 2688
NCTX = 256
NALL = NLOC + NCTX
NCH = NALL // 128
OWN0 = 384
EPS = 1e-6
NEG = -1e30
SCALE = 128 ** -0.5
DELTAS = [-6, -4, -2, 0, 2, 4, 6]
SPECIAL = [6, 8, 34, 36]

V_GMIX = 0
V_GFFN = 64
V_ADAB = 128
V_CBDW = 512
V_CLNG = 528
V_CLNB = 544
V_GFIN = 560
V_WDW = 576
V_S = 1072
V_VALIDB = 1104
V_CONVV = 1152
V_ALNG = 1154
V_ALNB = 1186
V_IDENT = 1218
NV = 1346


class Buf:
    __slots__ = ("name", "last_w", "readers", "semkey")

    def __init__(self, name):
        self.name = name
        self.last_w = None
        self.readers = []
        self.semkey = None


class Op:
    __slots__ = ("eng", "semkey", "count", "is_dma")

    def __init__(self, eng, semkey, count, is_dma):
        self.eng = eng
        self.semkey = semkey
        self.count = count
        self.is_dma = is_dma


ENGS = ("pe", "act", "dve", "pool", "sp")


def call(name, *a, **k):
    return (name, a, k)


class Prog:
    def __init__(self, nc, stack):
        self.nc = nc
        self.stack = stack
        self.streams = {e: [] for e in ENGS}
        self.sems = {}
        self.sem_total = {}
        self.seen = {e: {} for e in ENGS}
        for e in ENGS:
            self._sem("eng_" + e)
        self.n_dma_sems = 0

    def _sem(self, key):
        if key not in self.sems:
            self.sems[key] = self.stack.enter_context(self.nc.semaphore(key))
            self.sem_total[key] = 0
        return self.sems[key]

    def _need(self, eng, dep, acc):
        if dep is None:
            return
        if dep.is_dma:
            val = self.sem_total[dep.semkey]
        else:
            if dep.eng == "pe" and eng == "pe":
                return
            val = dep.count
        if acc.get(dep.semkey, 0) < val:
            acc[dep.semkey] = val

    def _flush(self, eng, acc):
        for key, val in acc.items():
            if self.seen[eng].get(key, 0) >= val:
                continue
            self.seen[eng][key] = val
            self.streams[eng].append(("wait", self.sems[key], val))

    def _deps(self, eng, reads, writes):
        acc = {}
        for b in reads:
            self._need(eng, b.last_w, acc)
        for b in writes:
            self._need(eng, b.last_w, acc)
            for r in b.readers:
                self._need(eng, r, acc)
        self._flush(eng, acc)

    def _commit(self, op, reads, writes):
        for b in reads:
            b.readers.append(op)
        for b in writes:
            b.last_w = op
            b.readers = []

    def op(self, eng, fn, reads=(), writes=()):
        self._deps(eng, reads, writes)
        key = "eng_" + eng
        self.sem_total[key] += 1
        o = Op(eng, key, self.sem_total[key], False)
        self.streams[eng].append(("op", fn, self.sems[key], 1))
        self._commit(o, reads, writes)
        return o

    def dma(self, queue, out, in_, reads, writes, sembuf):
        self._deps(queue, reads, writes)
        if sembuf.semkey is None:
            sembuf.semkey = "dma_%d" % self.n_dma_sems
            self.n_dma_sems += 1
            self._sem(sembuf.semkey)
        key = sembuf.semkey
        self.sem_total[key] += 16
        o = Op(queue, key, self.sem_total[key], True)
        self.streams[queue].append(("op", call('dma_start', out=out, in_=in_), self.sems[key], 16))
        self._commit(o, reads, writes)
        return o

    def final_wait(self, eng, bufs):
        acc = {}
        for b in bufs:
            self._need(eng, b.last_w, acc)
        self._flush(eng, acc)

    def emit(self, block):
        def run(name):
            def body(e):
                for it in self.streams[name]:
                    if it[0] == "wait":
                        e.wait_ge(it[1], it[2])
                    else:
                        f = it[1]
                        ins = getattr(e, f[0])(*f[1], **f[2]) if isinstance(f, tuple) else f(e)
                        ins.then_inc(it[2], it[3])
            return body
        block.sync(run("sp"))
        block.tensor(run("pe"))
        block.scalar(run("act"))
        block.vector(run("dve"))
        block.gpsimd(run("pool"))


class Builder:
    def __init__(self, layers, final):
        self.layers = layers
        self.final = final
        self.nc = bass.Bass("TRN2", target_bir_lowering=False)
        self.stack = contextlib.ExitStack()
        self.in_names = set()
        self.scrb = Buf("scr")
        self.smallb = Buf("small")

    def dram(self, name, shape, dt, kind):
        if kind == "ExternalInput":
            self.in_names.add(name)
        return self.nc.dram_tensor(name, list(shape), dt, kind=kind).ap()

    def sb(self, name, shape, dt):
        return self.stack.enter_context(self.nc.sbuf_tensor(name, list(shape), dt))

    def build(self):
        nc = self.nc
        with self.stack:
            self.P = Prog(nc, self.stack)
            self._declare()
            self._consts()
            self.bg_gen = None
            self.ffn_n = 0
            for l in self.layers:
                for _ in self._ada_gen(l):
                    pass
            for l in self.layers:
                [self._layer0, self._layer1, self._layer2, self._layer3][l](l)
            outs = [self.hb[self.layers[-1] + 1]] if not self.final else [self.outb]
            for bl in outs:
                self.P.final_wait("sp", bl)
            block = self.stack.enter_context(nc.Block())
            self.P.emit(block)
        nc._in_names = self.in_names
        return nc

    def _declare(self):
        L = self.layers
        ext_in = L[0]
        self.h = {}
        self.hb = {}
        for l in range(ext_in, L[-1] + 2):
            if l == 4:
                break
            if l == ext_in:
                kind = "ExternalInput"
            elif l == L[-1] + 1:
                kind = "ExternalOutput"
            else:
                kind = "Internal"
            self.h[l] = self.dram("h%d" % l, [D, NALL], F32, kind)
            self.hb[l] = [Buf("h%d_%d" % (l, c)) for c in range(NCH)]
        if self.final:
            self.out = self.dram("outT", [D, 2048], F32, "ExternalOutput")
            self.outb = [Buf("out_%d" % c) for c in range(16)]
        self.vecs_d = self.dram("vecs", [128, NV], F32, "ExternalInput")
        self.ada_w = {l: self.dram("ada_w%d" % l, [D, 6 * D], F32, "ExternalInput") for l in L}
        self.w1 = {l: self.dram("ffn_w1_%d" % l, [D, FF], F32, "ExternalInput") for l in L}
        self.w3 = {l: self.dram("ffn_w3_%d" % l, [D, FF], F32, "ExternalInput") for l in L}
        self.w2 = {l: self.dram("ffn_w2_%d" % l, [FF, D], F32, "ExternalInput") for l in L}
        self.a_w_in, self.a_w_out, self.a_wsT, self.a_bs = {}, {}, {}, {}
        for l_, ja in ((0, 0), (3, 1)):
            if l_ in L:
                self.a_w_in[ja] = self.dram("a_w_in%d" % ja, [D, 2 * D], F32, "ExternalInput")
                self.a_w_out[ja] = self.dram("a_w_out%d" % ja, [D, D], F32, "ExternalInput")
                self.a_wsT[ja] = self.dram("a_wsT%d" % ja, [128, 16 * 128], F32, "ExternalInput")
                self.a_bs[ja] = self.dram("a_b_s%d" % ja, [D], F32, "ExternalInput")
        if 1 in L:
            self.b_qkv = self.dram("b_w_qkv", [D, 3 * D], F32, "ExternalInput")
            self.b_out = self.dram("b_w_out", [D, D], F32, "ExternalInput")
            self.btab_d = self.dram("btab", [16, 128, 9 * 128], F32, "ExternalInput")
            self.q_d = self.dram("q_scr", [D, NALL], BF16, "Internal")
            self.k_d = self.dram("k_scr", [D, NALL], BF16, "Internal")
            self.v_d = self.dram("v_scr", [NALL, D], BF16, "Internal")
            self.qb = [Buf("q%d" % c) for c in range(NCH)]
            self.kb = [Buf("k%d" % c) for c in range(NCH)]
            self.vb = [Buf("v%d" % c) for c in range(NCH)]
        if 2 in L:
            self.c_pw1 = self.dram("c_w_pw1", [D, 2 * D], F32, "ExternalInput")
            self.c_pw2 = self.dram("c_w_pw2", [D, D], F32, "ExternalInput")
            self.y_d = self.dram("y_scr", [D, NALL], BF16, "Internal")
            self.yb = [Buf("y%d" % c) for c in range(NCH)]

        self.xT = self.sb("xT", [128, 16, 512], F32)
        self.xTb = Buf("xT")
        self.hT = self.sb("hT", [128, 16, 512], BF16)
        self.hTb = [Buf("hT%d" % i) for i in range(16)]
        self.scr = self.sb("scr", [128, 16640], F32)
        self.stg = [self.sb("stg%d" % i, [128, 4096], F32) for i in range(2)]
        self.stgh = [[Buf("stg%d_lo" % i), Buf("stg%d_hi" % i)] for i in range(2)]
        self.wb = [self.sb("wb%d" % i, [128, 4096], BF16) for i in range(2)]
        self.wbb = [Buf("wb%d" % i) for i in range(2)]
        self.ring = [(self.wb[0][:, :], [self.wbb[0]]), (self.wb[1][:, :], [self.wbb[1]])]
        for i in range(2):
            self.ring.append((self.stg[i][:, 0:2048].bitcast(BF16), [self.stgh[i][0]]))
            self.ring.append((self.stg[i][:, 2048:4096].bitcast(BF16), [self.stgh[i][1]]))
        self.ring_i = 0
        self.wcache = {}
        self.stg_i = 0
        self.wb_i = 0
        self.cast_i = 0
        self.vecs = self.sb("vecs_sb", [128, NV], F32)
        self.vecsb = Buf("vecs")
        self.mod = self.sb("mod", [128, 4, 96, 2], F32)
        self.modb = [Buf("mod%d" % l) for l in range(4)]
        self.Amod = self.sb("Amod", [128, 4, 2, 16, 2], F32)
        self.ssb = self.sb("silu_s", [128, 16, 2], F32)
        self.ssbb = Buf("silu_s")
        self.ssb16 = self.sb("silu_s16", [128, 16, 2], BF16)
        self.ones = self.sb("ones", [128, 128], BF16)
        self.onesb = Buf("ones")
        self.small = self.sb("small", [128, 64], F32)
        self.rstd = self.sb("rstd", [128, 512], F32)
        self.rstdb = Buf("rstd")
        self.tmpA = [self.sb("tmpA%d" % i, [128, 512], F32) for i in range(2)]
        self.tmpAb = [Buf("tmpA%d" % i) for i in range(2)]
        self.tmp_i = 0
        self.tmpB = [self.sb("tmpB%d" % i, [128, 512], BF16) for i in range(4)]
        self.tmpBb = [Buf("tmpB%d" % i) for i in range(4)]
        self.tmpB_i = 0
        self.ps = [self.stack.enter_context(nc_ps) for nc_ps in
                   [self.nc.psum_tensor("ps%d" % i, [128, 512], F32) for i in range(8)]]
        self.psb = [Buf("ps%d" % i) for i in range(8)]
        self.ps_i = 0
        self.lps_i = 0
        self.ident = self.sb("ident", [128, 128], BF16)
        self.tmpM = self.sb("tmpM", [128, 512], F32)
        self.tmpMb = Buf("tmpM")
        self.qsb = self.ksb = self.vsb = self.scrb
        self.lr_prev = []
        self.lr_cur = []
        self.LR = self.sb("LR", [128, 6144], F32)
        self.tmpA.append(self.LR[:, 5632:6144])
        self.tmpAb.append(Buf("tmpA2"))
        self.add2 = self.LR[:, 0:2048].rearrange("p (g q) -> p g q", g=16)
        self.wsT = self.LR[:, 2048:3072].bitcast(BF16)
        self.gcb = None

    def lr_begin(self):
        ops = []
        for b in self.lr_cur:
            if b.last_w is not None:
                ops.append(b.last_w)
            ops.extend(b.readers)
        self.lr_prev = ops
        self.lr_cur = []

    def lr_buf(self, name):
        b = Buf(name)
        b.readers = list(self.lr_prev)
        self.lr_cur.append(b)
        return b

    def bank(self):
        i = self.ps_i
        self.ps_i = (self.ps_i + 1) % 5
        return self.ps[i], self.psb[i]

    def lbank(self):
        i = 5 + self.lps_i
        self.lps_i = (self.lps_i + 1) % 3
        return self.ps[i], self.psb[i]

    def tmpa(self):
        i = self.tmp_i
        self.tmp_i = (i + 1) % 3
        return self.tmpA[i], self.tmpAb[i]

    def tmpb(self):
        i = self.tmpB_i
        self.tmpB_i = (i + 1) % 4
        return self.tmpB[i], self.tmpBb[i]

    def vcol(self, off, n=1):
        return self.vecs[:, off:off + n]

    def _consts(self):
        P = self.P
        P.dma("sp", self.vecs[:, :], self.vecs_d, [], [self.vecsb], self.vecsb)
        P.op("dve", call('memset', self.ones[:, :], 1.0), [], [self.onesb])
        P.op("dve", call('memset', self.small[:, 0:1], EPS), [], [self.onesb])
        P.op("dve", call('tensor_copy', out=self.ident[:, :], in_=self.vecs[:, V_IDENT:V_IDENT + 128]),
             [self.vecsb], [self.onesb])
        P.op("act", call('activation',
            out=self.ssb[:, :, :], in_=self.vecs[:, V_S:V_S + 32].rearrange("p (c k) -> p c k", k=2),
            func=AF.Silu), [self.vecsb], [self.ssbb])
        P.op("dve", call('tensor_copy', out=self.ssb16[:, :, :], in_=self.ssb[:, :, :]), [self.ssbb], [self.ssbb])

    def load_w(self, W, kc0, nk, c0, ncols, cast=True, cache=True):
        P = self.P
        n = nk * ncols
        ent = None
        if cast and cache:
            name = W.name
            ent = self.wcache.get(name)
            if ent is None:
                K_, N_ = W.shape
                ntiles = (K_ * N_) // (128 * n)
                ent = self.wcache[name] = {"ap": self.dram(name + "_bf", [ntiles, 128, n], BF16, "Internal"), "idx": {}}
            key = (kc0, nk, c0, ncols)
            if key in ent["idx"]:
                idx, cbuf = ent["idx"][key]
                ri = self.ring_i
                self.ring_i = (ri + 1) % len(self.ring)
                dst, bl = self.ring[ri]
                P.dma("sp", dst[:, 0:n], ent["ap"][idx], [cbuf], bl, bl[0])
                return dst[:, 0:n].rearrange("p (k n) -> p k n", k=nk), bl
        si = self.stg_i
        self.stg_i ^= 1
        bl = self.stgh[si]
        stg_ap = self.stg[si][:, 0:n].rearrange("p (k n) -> p k n", k=nk)
        src = W[kc0 * 128:(kc0 + nk) * 128, c0:c0 + ncols].rearrange("(k p) n -> p k n", p=128)
        P.dma("sp", stg_ap, src, [], bl, bl[0])
        if not cast:
            return stg_ap, bl
        wi = self.wb_i
        self.wb_i = (self.wb_i + 1) % 2
        eng = ("act", "dve")[self.cast_i % 2]
        self.cast_i += 1
        dst = self.wb[wi][:, 0:n]
        srcf = self.stg[si][:, 0:n]
        if eng == "act":
            P.op("act", call('activation', out=dst, in_=srcf, func=AF.Copy), bl, [self.wbb[wi]])
        else:
            P.op(eng, call('tensor_copy', out=dst, in_=srcf), bl, [self.wbb[wi]])
        if ent is None:
            return dst.rearrange("p (k n) -> p k n", k=nk), [self.wbb[wi]]
        idx = len(ent["idx"])
        cbuf = Buf("wc")
        ent["idx"][key] = (idx, cbuf)
        P.dma("pool" if eng == "pool" else "act", ent["ap"][idx], dst, [self.wbb[wi]], [cbuf], self.wbb[wi])
        return dst.rearrange("p (k n) -> p k n", k=nk), [self.wbb[wi]]

    def bg_step(self):
        if self.bg_gen is None:
            return
        try:
            next(self.bg_gen)
        except StopIteration:
            self.bg_gen = None

    def _ada_gen(self, l):
        P = self.P
        W = self.ada_w[l]
        for nb in range(24):
            if nb:
                yield
            ps, psb = self.bank()
            for kg in range(2):
                wt, wtb = self.load_w(W, kg * 8, 8, nb * 512, 512, cast=True, cache=False)
                for kc in range(8):
                    kk = kg * 8 + kc
                    P.op("pe", call('matmul', ps[0:2, :], self.ssb16[:, kk, :], wt[:, kc, :],
                                    start=(kk == 0), stop=(kk == 15)), wtb + [self.ssbb], [psb])
            t, tb = self.tmpa()
            P.op("dve", call('tensor_copy', out=t[0:2, :], in_=ps[0:2, :]), [psb], [tb])
            pt, ptb = self.bank()
            for i in range(4):
                P.op("pe", call('transpose', pt[:, 2 * i:2 * i + 2], t[0:2, i * 128:(i + 1) * 128],
                                self.vecs[0:2, V_IDENT:V_IDENT + 2]), [tb, self.vecsb], [ptb])
            a0 = V_ADAB + l * 96 + nb * 4
            P.op("dve", call('tensor_tensor', out=self.mod[:, l, nb * 4:nb * 4 + 4, :],
                             in0=pt[:, 0:8].rearrange("p (n k) -> p n k", k=2),
                             in1=self.vecs[:, a0:a0 + 4].unsqueeze(2).broadcast_to([128, 4, 2]), op=ALU.add),
                 [ptb, self.vecsb], [self.modb[l]])
        for which, (goff, sc0) in enumerate(((V_GMIX, 16), (V_GFFN, 64))):
            for k in range(2):
                P.op("dve", call('scalar_tensor_tensor',
                    out=self.Amod[:, l, which, :, k], in0=self.mod[:, l, sc0:sc0 + 16, k], scalar=1.0,
                    in1=self.vecs[:, goff + l * 16:goff + l * 16 + 16], op0=ALU.add, op1=ALU.mult),
                    [self.modb[l], self.vecsb], [self.modb[l]])

    def sc(self, l, name, dc, kind):
        if name == "A1":
            return self.Amod[:, l, 0, dc, kind:kind + 1]
        if name == "A2":
            return self.Amod[:, l, 1, dc, kind:kind + 1]
        base = {"sh1": 0, "gt1": 32, "sh2": 48, "gt2": 80}[name]
        return self.mod[:, l, base + dc, kind:kind + 1]

    @staticmethod
    def segs_T(segs):
        return sum(s[1] for s in segs)

    @staticmethod
    def seg_chunks(segs):
        out = []
        for c0, n, _ in segs:
            out.extend(range(c0 // 128, (c0 + n) // 128))
        return out

    def load_x(self, l, segs):
        P = self.P
        off = 0
        for c0, n, _ in segs:
            bl = [self.hb[l][c] for c in range(c0 // 128, (c0 + n) // 128)]
            P.dma("act", self.xT[:, :, off:off + n],
                  self.h[l][:, c0:c0 + n].rearrange("(c p) t -> p c t", p=128), bl, [self.xTb], self.xTb)
            off += n

    def store_x(self, l, segs):
        P = self.P
        off = 0
        for c0, n, _ in segs:
            bl = [self.hb[l][c] for c in range(c0 // 128, (c0 + n) // 128)]
            P.dma("act", self.h[l][:, c0:c0 + n].rearrange("(c p) t -> p c t", p=128),
                  self.xT[:, :, off:off + n], [self.xTb], bl, self.xTb)
            off += n

    def rms_stats(self, T):
        P = self.P
        ps, psb = self.bank()
        for dc in range(16):
            sq, sqb = self.tmpb()
            P.op("act", call('activation', out=sq[:, 0:T], in_=self.xT[:, dc, 0:T], func=AF.Square),
                 [self.xTb], [sqb])
            P.op("pe", call('matmul', ps[:, 0:T], self.ones[:, :], sq[:, 0:T],
                                                             start=(dc == 0), stop=(dc == 15)),
                 [sqb, self.onesb], [psb])
        P.op("act", call('activation', out=self.rstd[:, 0:T], in_=ps[:, 0:T], func=AF.Sqrt,
                                                  bias=self.small[:, 0:1], scale=1.0 / D),
             [psb, self.onesb], [self.rstdb])
        P.op("dve", call('reciprocal', out=self.rstd[:, 0:T], in_=self.rstd[:, 0:T]), [self.rstdb], [self.rstdb])

    def norm_mod(self, l, segs, An, shn):
        P = self.P
        T = self.segs_T(segs)
        self.rms_stats(T)
        for dc in range(16):
            off = 0
            for c0, n, kind in segs:
                t, tb = self.tmpa()
                P.op("dve", call('scalar_tensor_tensor',
                    out=t[:, 0:n], in0=self.xT[:, dc, off:off + n], scalar=self.sc(l, An, dc, kind),
                    in1=self.rstd[:, off:off + n], op0=ALU.mult, op1=ALU.mult),
                    [self.xTb, self.rstdb, self.modb[l]], [tb])
                P.op("act", call('activation',
                    out=self.hT[:, dc, off:off + n], in_=t[:, 0:n], func=AF.Identity,
                    bias=self.sc(l, shn, dc, kind), scale=1.0),
                    [tb, self.modb[l]], [self.hTb[dc]])
                off += n

    def resid_add(self, l, segs, gtn, dc, ps, psb):
        P = self.P
        off = 0
        for c0, n, kind in segs:
            P.op("dve", call('scalar_tensor_tensor',
                out=self.xT[:, dc, off:off + n], in0=ps[:, off:off + n], scalar=self.sc(l, gtn, dc, kind),
                in1=self.xT[:, dc, off:off + n], op0=ALU.mult, op1=ALU.add),
                [psb, self.xTb, self.modb[l]], [self.xTb])
            off += n

    def linear_ws(self, W, c0, ncols, rhs, rhsb, T, evac, KC=16):
        P = self.P
        for blk in range(ncols // 256):
            if KC == 16:
                wts = [self.load_w(W, 0, 16, c0 + blk * 256, 256)]
                per = 16
            else:
                wts = None
                per = 11
            banks = [self.bank() for _ in range(2)]
            ng = KC // per
            for kg in range(ng):
                if wts is not None:
                    wt, wtb = wts[0]
                else:
                    wt, wtb = self.load_w(W, kg * per, per, c0 + blk * 256, 256)
                for j in range(2):
                    ps, psb = banks[j]
                    for kc in range(per):
                        kk = kg * per + kc
                        P.op("pe", call('matmul',
                            ps[:, 0:T], wt[:, kc, j * 128:(j + 1) * 128], rhs[:, kk, 0:T],
                            start=(kk == 0), stop=(kk == KC - 1)),
                            wtb + [rhsb[kk] if isinstance(rhsb, list) else rhsb], [psb])
            for j in range(2):
                evac(blk * 2 + j, banks[j][0], banks[j][1])

    def ffn(self, l, segs):
        P = self.P
        T = self.segs_T(segs)
        gT = self.scr[:, 0:11264].bitcast(BF16).rearrange("p (c t) -> p c t", c=44)
        gTb = self.scrb
        self.norm_mod(l, segs, "A2", "sh2")
        W1, W3, W2 = self.w1[l], self.w3[l], self.w2[l]
        for blk in range(22):
            w1t, w1b = self.load_w(W1, 0, 16, blk * 256, 256)
            w3t, w3b = self.load_w(W3, 0, 16, blk * 256, 256)
            for j in range(2):
                fc = blk * 2 + j
                pa, pab = self.bank()
                pb, pbb = self.bank()
                for kc in range(16):
                    P.op("pe", call('matmul',
                        pa[:, 0:T], w1t[:, kc, j * 128:(j + 1) * 128], self.hT[:, kc, 0:T],
                        start=(kc == 0), stop=(kc == 15)), w1b + [self.hTb[kc]], [pab])
                for kc in range(16):
                    P.op("pe", call('matmul',
                        pb[:, 0:T], w3t[:, kc, j * 128:(j + 1) * 128], self.hT[:, kc, 0:T],
                        start=(kc == 0), stop=(kc == 15)), w3b + [self.hTb[kc]], [pbb])
                t, tb = self.tmpa()
                P.op("act", call('activation', out=t[:, 0:T], in_=pa[:, 0:T], func=AF.Silu), [pab], [tb])
                P.op("dve", call('tensor_tensor',
                    out=gT[:, fc, 0:T], in0=t[:, 0:T], in1=pb[:, 0:T], op=ALU.mult), [tb, pbb], [gTb])
            if self.ffn_n > 0:
                self.bg_step()
        self.ffn_n += 1
        self.linear_ws(W2, 0, D, gT, gTb, T,
                       lambda j, ps, psb: self.resid_add(l, segs, "gt2", j, ps, psb), KC=44)

    def gmlp_consts(self, ja):
        P = self.P
        self.lr_begin()
        self.gcb = self.lr_buf("gmlp_consts")
        si = self.stg_i
        self.stg_i ^= 1
        P.dma("sp", self.stg[si][:, 0:D], self.a_wsT[ja], [], self.stgh[si], self.stgh[si][0])
        P.op("dve", call('tensor_copy', out=self.wsT[:, :], in_=self.stg[si][:, 0:D]), self.stgh[si], [self.gcb])
        s2 = self.stg_i
        self.stg_i ^= 1
        P.dma("sp", self.stg[s2][:, 0:D], self.a_bs[ja].partition_broadcast(128), [], self.stgh[s2], self.stgh[s2][0])
        for gq in range(4):
            ps, psb = self.bank()
            P.op("pe", call('matmul', ps[:, :], self.ones[:, :], self.wsT[:, gq * 512:(gq + 1) * 512],
                                                       start=True, stop=True), [self.gcb, self.onesb], [psb])
            for gi in range(4):
                g = gq * 4 + gi
                P.op("dve", call('scalar_tensor_tensor',
                    out=self.add2[:, g, :], in0=ps[:, gi * 128:(gi + 1) * 128], scalar=self.vcol(V_ALNB + ja * 16 + g),
                    in1=self.stg[s2][:, g * 128:(g + 1) * 128], op0=ALU.mult, op1=ALU.add),
                    [psb, self.vecsb] + self.stgh[s2], [self.gcb])

    def gmlp(self, l, ja, segs):
        P = self.P
        T = self.segs_T(segs)
        ns = T // 128
        scrb = self.scrb
        vtok = self.scr[:, 0:8192].rearrange("p (s d) -> p s d", s=4)
        uT = self.scr[:, 8192:12288].bitcast(BF16).rearrange("p (c t) -> p c t", c=16)
        vln = self.scr[:, 12288:16384].bitcast(BF16).rearrange("p (s d) -> p s d", s=4)
        Win = self.a_w_in[ja]
        self.norm_mod(l, segs, "A1", "sh1")
        self.linear_ws(Win, 0, D, self.hT, self.hTb, T,
                       lambda j, ps, psb: P.op("act", call('activation',
                           out=uT[:, j, 0:T], in_=ps[:, 0:T], func=AF.Gelu), [psb], [scrb]))
        for nb in range(4):
            banks = [self.bank() for _ in range(ns)]
            for kg in range(2):
                wt, wtb = self.load_w(Win, kg * 8, 8, D + nb * 512, 512)
                for s in range(ns):
                    ps, psb = banks[s]
                    for kc in range(8):
                        kk = kg * 8 + kc
                        P.op("pe", call('matmul',
                            ps[:, :], self.hT[:, kk, s * 128:(s + 1) * 128], wt[:, kc, :],
                            start=(kk == 0), stop=(kk == 15)), wtb + [self.hTb[kk]], [psb])
            for s in range(ns):
                ps, psb = banks[s]
                P.op("act", call('activation',
                    out=vtok[:, s, nb * 512:(nb + 1) * 512], in_=ps[:, :], func=AF.Gelu), [psb], [scrb])
        st = self.small
        for s in range(ns):
            for q in range(4):
                P.op("dve", call('bn_stats', out=st[:, 8 + q * 6:14 + q * 6], in_=vtok[:, s, q * 512:(q + 1) * 512]),
                     [scrb], [self.smallb])
            P.op("dve", call('bn_aggr', out=st[:, 2:4], in_=st[:, 8:32]), [self.smallb], [self.smallb])
            P.op("act", call('activation', out=st[:, 4:5], in_=st[:, 3:4], func=AF.Sqrt, bias=st[:, 0:1], scale=1.0),
                 [self.smallb, self.onesb], [self.smallb])
            P.op("dve", call('reciprocal', out=st[:, 5:6], in_=st[:, 4:5]), [self.smallb], [self.smallb])
            P.op("dve", call('tensor_scalar', out=vln[:, s, :], in0=vtok[:, s, :], scalar1=st[:, 2:3],
                                                       scalar2=st[:, 5:6], op0=ALU.subtract, op1=ALU.mult),
                 [scrb, self.smallb], [scrb])
        for s in range(ns):
            for gq in range(4):
                ps, psb = self.bank()
                for gi in range(4):
                    g = gq * 4 + gi
                    P.op("pe", call('matmul',
                        ps[:, gi * 128:(gi + 1) * 128], vln[:, s, g * 128:(g + 1) * 128], self.wsT[:, g * 128:(g + 1) * 128],
                        start=True, stop=True), [scrb, self.gcb], [psb])
                t, tb = self.tmpa()
                for gi in range(4):
                    g = gq * 4 + gi
                    P.op("dve", call('scalar_tensor_tensor',
                        out=t[:, gi * 128:(gi + 1) * 128], in0=ps[:, gi * 128:(gi + 1) * 128],
                        scalar=self.vcol(V_ALNG + ja * 16 + g), in1=self.add2[:, g, :], op0=ALU.mult, op1=ALU.add),
                        [psb, self.gcb, self.vecsb], [tb])
                P.op("pool", call('tensor_tensor',
                    out=self.hT[:, gq * 4:(gq + 1) * 4, s * 128:(s + 1) * 128],
                    in0=t[:, :].rearrange("p (g q) -> p g q", g=4),
                    in1=uT[:, gq * 4:(gq + 1) * 4, s * 128:(s + 1) * 128], op=ALU.mult),
                    [tb, scrb], self.hTb[gq * 4:gq * 4 + 4])
        self.linear_ws(self.a_w_out[ja], 0, D, self.hT, self.hTb, T,
                       lambda j, ps, psb: self.resid_add(l, segs, "gt1", j, ps, psb))

    def _layer0(self, l):
        self._gmlp_layer(l, 0, [[(0, 384, 0)], [(384, 512, 0)], [(896, 512, 0)], [(1408, 512, 0)], [(1920, 512, 0)],
                                [(2432, 256, 0), (2688, 256, 1)]])

    def _layer3(self, l):
        self._gmlp_layer(l, 1, [[(384 + 512 * i, 512, 0)] for i in range(4)])

    def _gmlp_layer(self, l, ja, tiles):
        self.gmlp_consts(ja)
        for segs in tiles:
            self.load_x(l, segs)
            self.gmlp(l, ja, segs)
            self.ffn(l, segs)
            if l == 3 and self.final:
                self.final_norm(segs)
            else:
                self.store_x(l + 1, segs)

    def final_norm(self, segs):
        P = self.P
        T = self.segs_T(segs)
        self.rms_stats(T)
        for dc in range(16):
            P.op("dve", call('scalar_tensor_tensor',
                out=self.xT[:, dc, 0:T], in0=self.xT[:, dc, 0:T], scalar=self.vcol(V_GFIN + dc),
                in1=self.rstd[:, 0:T], op0=ALU.mult, op1=ALU.mult), [self.xTb, self.rstdb, self.vecsb], [self.xTb])
        c0 = segs[0][0] - OWN0
        bl = [self.outb[c] for c in range(c0 // 128, (c0 + T) // 128)]
        P.dma("act", self.out[:, c0:c0 + T].rearrange("(c p) t -> p c t", p=128), self.xT[:, :, 0:T],
              [self.xTb], bl, self.xTb)

    @staticmethod
    def pair_cks(lr0):
        if lr0 == 6:
            return list(range(1, 7))
        if lr0 == 36:
            return list(range(15, 21))
        return [ck for ck in range((lr0 - 4) // 2, (lr0 + 4) // 2 + 1) if ck <= 20]

    def _layer1(self, l):
        P = self.P
        Wq = self.b_qkv
        qs = self.scr[:, 0:4096].bitcast(BF16).rearrange("p (c t) -> p c t", c=16)
        ks = self.scr[:, 4096:8192].bitcast(BF16).rearrange("p (c t) -> p c t", c=16)
        vs = self.scr[:, 8192:12288].bitcast(BF16).rearrange("p (s d) -> p s d", s=4)
        tiles = [[(512 * i, 512, 0)] for i in range(5)] + [[(2560, 128, 0), (2688, 256, 1)]]
        for segs in tiles:
            T = self.segs_T(segs)
            ns = T // 128
            self.load_x(l, segs)
            self.norm_mod(l, segs, "A1", "sh1")
            self.linear_ws(Wq, 0, D, self.hT, self.hTb, T,
                           lambda j, ps, psb: P.op("act", call('activation',
                               out=qs[:, j, 0:T], in_=ps[:, 0:T], func=AF.Copy), [psb], [self.qsb]))
            self.linear_ws(Wq, D, D, self.hT, self.hTb, T,
                           lambda j, ps, psb: P.op("dve", call('tensor_copy',
                               out=ks[:, j, 0:T], in_=ps[:, 0:T]), [psb], [self.ksb]))
            for nb in range(4):
                banks = [self.bank() for _ in range(ns)]
                for kg in range(2):
                    wt, wtb = self.load_w(Wq, kg * 8, 8, 2 * D + nb * 512, 512)
                    for s in range(ns):
                        ps, psb = banks[s]
                        for kc in range(8):
                            kk = kg * 8 + kc
                            P.op("pe", call('matmul',
                                ps[:, :], self.hT[:, kk, s * 128:(s + 1) * 128], wt[:, kc, :],
                                start=(kk == 0), stop=(kk == 15)), wtb + [self.hTb[kk]], [psb])
                for s in range(ns):
                    ps, psb = banks[s]
                    P.op("act", call('activation',
                        out=vs[:, s, nb * 512:(nb + 1) * 512], in_=ps[:, :], func=AF.Copy), [psb], [self.vsb])
            off = 0
            for c0, n, _ in segs:
                chs = list(range(c0 // 128, (c0 + n) // 128))
                P.dma("act", self.q_d[:, c0:c0 + n].rearrange("(c p) t -> p c t", p=128), qs[:, :, off:off + n],
                      [self.qsb], [self.qb[c] for c in chs], self.qsb)
                P.dma("act", self.k_d[:, c0:c0 + n].rearrange("(c p) t -> p c t", p=128), ks[:, :, off:off + n],
                      [self.ksb], [self.kb[c] for c in chs], self.ksb)
                P.dma("act", self.v_d[c0:c0 + n, :].rearrange("(s p) d -> p s d", p=128),
                      vs[:, off // 128:(off + n) // 128, :], [self.vsb], [self.vb[c] for c in chs], self.vsb)
                off += n
        LRb = self.LR[:, :].bitcast(BF16)
        self.lr_begin()
        kh, vh, qh, bt, slotb = [], [], [], [], []
        for sl in range(2):
            base = sl * 2688
            kh.append(LRb[:, 2 * base:2 * base + 1280])
            vh.append(LRb[:, 2 * base + 1280:2 * base + 2560].rearrange("p (c d) -> p c d", c=10))
            qh.append(LRb[:, 2 * base + 2560:2 * base + 3072])
            bt.append(self.LR[:, base + 1536:base + 2688].rearrange("p (s q) -> p s q", s=9))
            slotb.append(self.lr_buf("aslot%d" % sl))
        pT = [LRb[:, 10752 + i * 128:10752 + (i + 1) * 128] for i in range(4)]
        pTb = [self.lr_buf("pT%d" % i) for i in range(4)]
        pti = [0]
        vbv = self.vecs[:, V_VALIDB:V_VALIDB + 48].rearrange("p (s c r) -> p s c r", s=4, c=6)
        tiles = [[(256 + 512 * i, 512, 0)] for i in range(4)] + [[(2304, 256, 0), (2688, 256, 1)]]

        att_t = [self.scr[:, 12288 + i * 128:12288 + (i + 1) * 128] for i in range(16)]
        att_tb = [Buf("att_t%d" % i) for i in range(16)]
        att_pv = self.scr[:, 14336:15360].bitcast(BF16)
        att_p = [att_pv[:, i * 128:(i + 1) * 128] for i in range(16)]
        att_pb = [Buf("att_p%d" % i) for i in range(16)]
        par = [0]
        pend = [None]

        def attend(h, sl, qcol0, blocks, lr0, ck_lo, ocol0):
            nb_ = len(blocks)
            base = 8 * par[0]
            par[0] ^= 1
            qcols = qh[sl][:, qcol0:qcol0 + 128]
            s_aps, s_bufs, vblks = [], [], []
            for bi, (ck, kind) in enumerate(blocks):
                if bi % 4 == 0:
                    s_ps, s_psb = self.bank()
                if kind == 0:
                    kcols = kh[sl][:, (ck - ck_lo) * 128:(ck - ck_lo + 1) * 128]
                    vblks.append(vh[sl][:, ck - ck_lo, :])
                else:
                    kcols = kh[sl][:, 1024 + ck * 128:1024 + (ck + 1) * 128]
                    vblks.append(vh[sl][:, 8 + ck, :])
                s_ap = s_ps[:, (bi % 4) * 128:(bi % 4 + 1) * 128]
                s_aps.append(s_ap)
                s_bufs.append(s_psb)
                P.op("pe", call('matmul', s_ap, kcols, qcols, start=True, stop=True), [slotb[sl]], [s_psb])
            for bi, (ck, kind) in enumerate(blocks):
                s_ap, s_psb = s_aps[bi], s_bufs[bi]
                p_ap, p_b = att_p[base + bi], att_pb[base + bi]
                if kind == 0:
                    delta = 2 * ck - lr0
                    special = lr0 in SPECIAL
                    if special or delta not in (-4, 4):
                        idx = DELTAS.index(delta)
                    else:
                        idx = 7 if delta == -4 else 8
                    t, tb = att_t[base + bi], att_tb[base + bi]
                    P.op("dve", call('scalar_tensor_tensor', out=t, in0=s_ap, scalar=SCALE, in1=bt[sl][:, idx, :],
                                     op0=ALU.mult, op1=ALU.add), [s_psb, slotb[sl]], [tb])
                    if special:
                        si = SPECIAL.index(lr0)
                        ci = self.pair_cks(lr0).index(ck)
                        for rq in range(2):
                            P.op("act", call('activation', out=p_ap[:, rq * 64:(rq + 1) * 64],
                                             in_=t[:, rq * 64:(rq + 1) * 64], func=AF.Exp,
                                             bias=vbv[:, si, ci, rq:rq + 1], scale=1.0), [tb, self.vecsb], [p_b])
                    else:
                        P.op("act", call('activation', out=p_ap, in_=t, func=AF.Exp), [tb], [p_b])
                else:
                    P.op("act", call('activation', out=p_ap, in_=s_ap, func=AF.Exp, scale=SCALE), [s_psb], [p_b])
            prev = pend[0]
            pend[0] = (h, sl, nb_, base, vblks, ocol0)
            if prev is not None:
                attend_pv(*prev)

        def attend_pv(h, sl, nb_, base, vblks, ocol0):
            o_ps, o_psb = self.lbank()
            d_ps, d_psb = self.lbank()
            for bi in range(nb_):
                p_ap, p_b = att_p[base + bi], att_pb[base + bi]
                P.op("pe", call('matmul', o_ps[:, 0:128], vblks[bi], p_ap, start=(bi == 0), stop=(bi == nb_ - 1)),
                     [slotb[sl], p_b], [o_psb])
                P.op("pe", call('matmul', d_ps[:, 0:128], self.ones[:, :], p_ap, start=(bi == 0), stop=(bi == nb_ - 1)),
                     [self.onesb, p_b], [d_psb])
            t, tb = self.tmpa()
            P.op("dve", call('reciprocal', out=t[:, 0:128], in_=d_ps[:, 0:128]), [d_psb], [tb])
            P.op("dve", call('tensor_tensor', out=self.hT[:, h, ocol0:ocol0 + 128], in0=o_ps[:, 0:128], in1=t[:, 0:128],
                             op=ALU.mult), [o_psb, tb], [self.hTb[h]])

        for segs in tiles:
            T = self.segs_T(segs)
            self.load_x(l, segs)
            lat = segs[0]
            npairs = lat[1] // 128
            pairs = [lat[0] // 64 + 2 * i for i in range(npairs)]
            allck = [ck for p_ in pairs for ck in self.pair_cks(p_)]
            ck_lo, ck_hi = min(allck), max(allck)
            nck = ck_hi - ck_lo + 1
            assert nck <= 8
            has_ctx = len(segs) > 1
            for h in range(16):
                sl = h % 2
                hr = slice(h * 128, (h + 1) * 128)
                kbl = [self.kb[c] for c in range(ck_lo, ck_hi + 1)] + [self.kb[21], self.kb[22]]
                vbl = [self.vb[c] for c in range(ck_lo, ck_hi + 1)] + [self.vb[21], self.vb[22]]
                P.dma("sp", kh[sl][:, 0:nck * 128], self.k_d[hr, ck_lo * 128:(ck_hi + 1) * 128], kbl, [slotb[sl]], slotb[sl])
                P.dma("sp", kh[sl][:, 1024:1280], self.k_d[hr, NLOC:NALL], kbl, [slotb[sl]], slotb[sl])
                P.dma("sp", vh[sl][:, 0:nck, :], self.v_d[ck_lo * 128:(ck_hi + 1) * 128, hr].rearrange("(c p) d -> p c d", p=128),
                      vbl, [slotb[sl]], slotb[sl])
                P.dma("sp", vh[sl][:, 8:10, :], self.v_d[NLOC:NALL, hr].rearrange("(c p) d -> p c d", p=128),
                      vbl, [slotb[sl]], slotb[sl])
                off = 0
                for c0, n, _ in segs:
                    P.dma("sp", qh[sl][:, off:off + n], self.q_d[hr, c0:c0 + n],
                          [self.qb[c] for c in range(c0 // 128, (c0 + n) // 128)], [slotb[sl]], slotb[sl])
                    off += n
                P.dma("sp", bt[sl], self.btab_d[h].rearrange("p (s q) -> p s q", s=9), [], [slotb[sl]], slotb[sl])
                for pi, lr0 in enumerate(pairs):
                    blocks = [(ck, 0) for ck in self.pair_cks(lr0)] + [(0, 1), (1, 1)]
                    attend(h, sl, pi * 128, blocks, lr0, ck_lo, pi * 128)
                if has_ctx:
                    for cp in range(2):
                        attend(h, sl, lat[1] + cp * 128, [(0, 1), (1, 1)], None, ck_lo, lat[1] + cp * 128)
            if pend[0] is not None:
                attend_pv(*pend[0])
                pend[0] = None
            self.linear_ws(self.b_out, 0, D, self.hT, self.hTb, T,
                           lambda j, ps, psb: self.resid_add(l, segs, "gt1", j, ps, psb))
            self.ffn(l, segs)
            self.store_x(l + 1, segs)

    def _layer2(self, l):
        P = self.P
        ys = self.scr[:, 0:4096].bitcast(BF16).rearrange("p (c t) -> p c t", c=16)
        ysb = self.scrb
        W1 = self.c_pw1
        tiles = [[(256 + 512 * i, 512, 0)] for i in range(4)] + [[(2304, 256, 0), (2688, 256, 1)]]
        for segs in tiles:
            T = self.segs_T(segs)
            self.load_x(l, segs)
            self.norm_mod(l, segs, "A1", "sh1")
            for blk in range(8):
                wa, wab = self.load_w(W1, 0, 16, blk * 256, 256)
                wg, wgb = self.load_w(W1, 0, 16, D + blk * 256, 256)
                for j in range(2):
                    pa, pab = self.bank()
                    pg, pgb = self.bank()
                    for kc in range(16):
                        P.op("pe", call('matmul',
                            pa[:, 0:T], wa[:, kc, j * 128:(j + 1) * 128], self.hT[:, kc, 0:T],
                            start=(kc == 0), stop=(kc == 15)), wab + [self.hTb[kc]], [pab])
                    for kc in range(16):
                        P.op("pe", call('matmul',
                            pg[:, 0:T], wg[:, kc, j * 128:(j + 1) * 128], self.hT[:, kc, 0:T],
                            start=(kc == 0), stop=(kc == 15)), wgb + [self.hTb[kc]], [pgb])
                    t, tb = self.tmpa()
                    P.op("act", call('activation', out=t[:, 0:T], in_=pg[:, 0:T], func=AF.Sigmoid), [pgb], [tb])
                    P.op("dve", call('tensor_tensor',
                        out=ys[:, blk * 2 + j, 0:T], in0=pa[:, 0:T], in1=t[:, 0:T], op=ALU.mult), [pab, tb], [ysb])
            off = 0
            for c0, n, _ in segs:
                P.dma("act", self.y_d[:, c0:c0 + n].rearrange("(c p) t -> p c t", p=128), ys[:, :, off:off + n],
                      [ysb], [self.yb[c] for c in range(c0 // 128, (c0 + n) // 128)], ysb)
                off += n
        ywin = self.scr[:, 4096:8448].bitcast(BF16).rearrange("p (c t) -> p c t", c=16)
        ywb = self.scrb
        z = self.scr[:, 8448:16640].rearrange("p (c t) -> p c t", c=16)
        zb_ = self.scrb
        LRb = self.LR[:, :].bitcast(BF16)
        wdiag = [LRb[:, i * 3968:(i + 1) * 3968].rearrange("p (j m) -> p j m", j=31) for i in range(2)]
        self.lr_begin()
        wdb = [self.lr_buf("wdiag%d" % i) for i in range(2)]
        wdw = self.vecs[:, V_WDW:V_WDW + 496].rearrange("p (c j) -> p c j", c=16)
        tiles = [[(384 + 512 * i, 512, 0)] for i in range(4)] + [[(2688, 256, 1)]]
        for segs in tiles:
            c0, T, kind = segs[0]
            self.load_x(l, segs)
            if kind == 0:
                P.dma("sp", ywin[:, :, 0:T + 30], self.y_d[:, c0 - 15:c0 + T + 15].rearrange("(c p) t -> p c t", p=128),
                      [self.yb[c] for c in range((c0 - 15) // 128, (c0 + T + 14) // 128 + 1)], [ywb], ywb)
                if c0 == 384:
                    P.op("dve", call('tensor_scalar', out=ywin[:, :, 0:15], in0=ywin[:, :, 0:15],
                                                          scalar1=self.vcol(V_CONVV), scalar2=None, op0=ALU.mult),
                         [ywb, self.vecsb], [ywb])
                if c0 == 1920:
                    P.op("dve", call('tensor_scalar', out=ywin[:, :, T + 15:T + 30], in0=ywin[:, :, T + 15:T + 30],
                                                          scalar1=self.vcol(V_CONVV + 1), scalar2=None, op0=ALU.mult),
                         [ywb, self.vecsb], [ywb])
            else:
                P.op("dve", call('memset', ywin[:, :, 0:15], 0.0), [], [ywb])
                P.op("dve", call('memset', ywin[:, :, T + 15:T + 30], 0.0), [], [ywb])
                P.dma("sp", ywin[:, :, 15:15 + T], self.y_d[:, c0:c0 + T].rearrange("(c p) t -> p c t", p=128),
                      [self.yb[21], self.yb[22]], [ywb], ywb)
            m_ps, m_psb = self.lbank()
            e_ps, e_psb = self.lbank()
            for dc in range(16):
                wi = dc % 2
                P.op("pool", call('tensor_tensor',
                    out=wdiag[wi], in0=self.ident[:, :].unsqueeze(1).broadcast_to([128, 31, 128]),
                    in1=wdw[:, dc, :].unsqueeze(2).broadcast_to([128, 31, 128]), op=ALU.mult),
                    [self.onesb, self.vecsb], [wdb[wi]])
                c_ps, c_psb = self.bank()
                for j in range(31):
                    P.op("pe", call('matmul',
                        c_ps[:, 0:T], wdiag[wi][:, j, :], ywin[:, dc, j:j + T], start=(j == 0), stop=(j == 30)),
                        [wdb[wi], ywb], [c_psb])
                bcol = self.vcol(V_CBDW + dc)
                P.op("act", call('activation',
                    out=z[:, dc, 0:T], in_=c_ps[:, 0:T], func=AF.Identity, bias=bcol, scale=1.0), [c_psb, self.vecsb], [zb_])
                zb16, zb16b = self.tmpb()
                P.op("dve", call('tensor_copy', out=zb16[:, 0:T], in_=z[:, dc, 0:T]), [zb_], [zb16b])
                z2, z2b = self.tmpb()
                P.op("act", call('activation', out=z2[:, 0:T], in_=z[:, dc, 0:T], func=AF.Square), [zb_], [z2b])
                P.op("pe", call('matmul', m_ps[:, 0:T], self.ones[:, :], zb16[:, 0:T],
                                                                start=(dc == 0), stop=(dc == 15)), [zb16b, self.onesb], [m_psb])
                P.op("pe", call('matmul', e_ps[:, 0:T], self.ones[:, :], z2[:, 0:T],
                                                            start=(dc == 0), stop=(dc == 15)), [z2b, self.onesb], [e_psb])
            mt = self.tmpM
            P.op("act", call('activation', out=mt[:, 0:T], in_=m_ps[:, 0:T], func=AF.Copy, scale=1.0 / D), [m_psb], [self.tmpMb])
            t, tb = self.tmpa()
            P.op("dve", call('tensor_tensor', out=t[:, 0:T], in0=mt[:, 0:T], in1=mt[:, 0:T], op=ALU.mult), [self.tmpMb], [tb])
            P.op("dve", call('scalar_tensor_tensor', out=self.rstd[:, 0:T], in0=e_ps[:, 0:T], scalar=1.0 / D,
                                                              in1=t[:, 0:T], op0=ALU.mult, op1=ALU.subtract),
                 [e_psb, tb], [self.rstdb])
            P.op("act", call('activation', out=self.rstd[:, 0:T], in_=self.rstd[:, 0:T], func=AF.Sqrt,
                                               bias=self.small[:, 0:1], scale=1.0), [self.rstdb, self.onesb], [self.rstdb])
            P.op("dve", call('reciprocal', out=self.rstd[:, 0:T], in_=self.rstd[:, 0:T]), [self.rstdb], [self.rstdb])
            for dc in range(16):
                t, tb = self.tmpa()
                P.op("dve", call('tensor_tensor', out=t[:, 0:T], in0=z[:, dc, 0:T], in1=mt[:, 0:T], op=ALU.subtract),
                     [zb_, self.tmpMb], [tb])
                P.op("dve", call('tensor_tensor', out=t[:, 0:T], in0=t[:, 0:T], in1=self.rstd[:, 0:T], op=ALU.mult),
                     [tb, self.rstdb], [tb])
                P.op("act", call('activation',
                    out=self.hT[:, dc, 0:T], in_=t[:, 0:T], func=AF.Silu, bias=self.vcol(V_CLNB + dc),
                    scale=self.vcol(V_CLNG + dc)), [tb, self.vecsb], [self.hTb[dc]])
            self.linear_ws(self.c_pw2, 0, D, self.hT, self.hTb, T,
                           lambda j, ps, psb: self.resid_add(l, segs, "gt1", j, ps, psb))
            self.ffn(l, segs)
            self.store_x(l + 1, segs)


def _fm(v):
    return np.ascontiguousarray(np.asarray(v, np.float32).reshape(16, 128).T)


def _core_inputs(inp, core):
    b, q = core // 4, core % 4
    x = inp["x"][b]
    g0 = 2048 * q - OWN0
    loc = np.zeros((NALL, D), np.float32)
    lo, hi = max(g0, 0), min(g0 + NLOC, 8192)
    loc[lo - g0:hi - g0] = x[lo:hi]
    loc[NLOC:] = inp["ctx"][b]
    h0 = np.ascontiguousarray(loc.T)
    vecs = np.zeros((128, NV), np.float32)
    for l in range(4):
        vecs[:, V_GMIX + 16 * l:V_GMIX + 16 * l + 16] = _fm(inp["g_mix"][l])
        vecs[:, V_GFFN + 16 * l:V_GFFN + 16 * l + 16] = _fm(inp["g_ffn"][l])
        vecs[:, V_ADAB + 96 * l:V_ADAB + 96 * l + 96] = np.asarray(inp["ada_b"][l], np.float32).reshape(96, 128).T
    vecs[:, V_CBDW:V_CBDW + 16] = _fm(inp["c_b_dw"][0])
    vecs[:, V_CLNG:V_CLNG + 16] = _fm(inp["c_ln_g"][0])
    vecs[:, V_CLNB:V_CLNB + 16] = _fm(inp["c_ln_b"][0])
    vecs[:, V_GFIN:V_GFIN + 16] = _fm(inp["g_final"])
    wdw = np.asarray(inp["c_w_dw"][0], np.float32)
    vecs[:, V_WDW:V_WDW + 496] = wdw.reshape(31, 16, 128).transpose(2, 1, 0).reshape(128, 496)
    s = np.stack([_fm(inp["c"][b]), _fm(inp["c_ctx"])], axis=-1)
    vecs[:, V_S:V_S + 32] = s.reshape(128, 32)
    vb = np.zeros((128, 4, 6, 2), np.float32)
    for si, lr0 in enumerate(SPECIAL):
        cks = {6: range(1, 7), 8: range(2, 7), 34: range(15, 20), 36: range(15, 21)}[lr0]
        for ci, ck in enumerate(cks):
            for rq in range(2):
                lr = lr0 + rq
                gr = 32 * q - 6 + lr
                if 0 <= gr < 128:
                    ws = min(max(gr - 4, 0), 120) - (32 * q - 6)
                else:
                    ws = lr - 4
                for rk in range(2):
                    kr = 2 * ck + rk
                    if not (ws <= kr < ws + 8):
                        vb[rk * 64:(rk + 1) * 64, si, ci, rq] = NEG
    vecs[:, V_VALIDB:V_VALIDB + 48] = vb.reshape(128, 48)
    for ja in range(2):
        vecs[:, V_ALNG + 16 * ja:V_ALNG + 16 * ja + 16] = _fm(inp["a_ln_g"][ja])
        vecs[:, V_ALNB + 16 * ja:V_ALNB + 16 * ja + 16] = _fm(inp["a_ln_b"][ja])
    vecs[:, V_IDENT:V_IDENT + 128] = np.eye(128, dtype=np.float32)
    vecs[:, V_CONVV] = 0.0 if q == 0 else 1.0
    vecs[:, V_CONVV + 1] = 0.0 if q == 3 else 1.0
    return h0, vecs


def _btab(rpb):
    rpb = np.asarray(rpb, np.float32)
    cols = np.arange(64)
    cstart = np.clip(cols - 8, 0, 48)
    kc = cols[:, None]
    c = cols[None, :]
    col_in = (kc >= cstart[None, :]) & (kc < cstart[None, :] + 16)
    dc_idx = np.clip(kc - c + 15, 0, 30)
    B = np.where(col_in[None, None], rpb[:, :, dc_idx], np.float32(NEG)).astype(np.float32)
    out = np.full((16, 128, 9, 128), NEG, np.float32)
    slots = [(d, False) for d in DELTAS] + [(-4, True), (4, True)]
    for si, (dl, baked) in enumerate(slots):
        for rk in range(2):
            for rq in range(2):
                dr = dl + rk - rq
                if baked and not (-4 <= dr <= 3):
                    continue
                out[:, rk * 64:(rk + 1) * 64, si, rq * 64:(rq + 1) * 64] = B[:, dr + 7]
    return out.reshape(16, 128, 9 * 128)


_NC_CACHE = {}


def _get_nc(layers, final):
    key = (tuple(layers), final)
    if key not in _NC_CACHE:
        _NC_CACHE[key] = Builder(list(layers), final).build()
    return _NC_CACHE[key]


GROUPS = [[0, 1, 2, 3]]


def shared_inputs(inp):
    sh = {}
    for l in range(4):
        sh["ada_w%d" % l] = inp["ada_w"][l]
        sh["ffn_w1_%d" % l] = inp["ffn_w1"][l]
        sh["ffn_w3_%d" % l] = inp["ffn_w3"][l]
        sh["ffn_w2_%d" % l] = inp["ffn_w2"][l]
    for ja in range(2):
        sh["a_w_in%d" % ja] = inp["a_w_in"][ja]
        sh["a_w_out%d" % ja] = inp["a_w_out"][ja]
        sh["a_wsT%d" % ja] = np.ascontiguousarray(inp["a_w_s"][ja].transpose(2, 0, 1).reshape(128, 2048))
        sh["a_b_s%d" % ja] = np.ascontiguousarray(inp["a_b_s"][ja].reshape(2048))
    sh.update({
        "b_w_qkv": inp["b_w_qkv"][0], "b_w_out": inp["b_w_out"][0], "btab": _btab(inp["b_rpb"][0]),
        "c_w_pw1": inp["c_w_pw1"][0], "c_w_pw2": inp["c_w_pw2"][0],
    })
    return sh


def run_groups(inp, groups, want_h=False, hcur0=None):
    inp = {k: np.asarray(v) for k, v in inp.items()}
    shared = shared_inputs(inp)
    per_core = [_core_inputs(inp, c) for c in range(8)]
    hcur = hcur0 if hcur0 is not None else [pc[0] for pc in per_core]
    res = None
    hs = []
    for layers in groups:
        final = layers[-1] == 3
        nc = _get_nc(layers, final)
        in_maps = []
        for c in range(8):
            m = {"h%d" % layers[0]: hcur[c], "vecs": per_core[c][1]}
            m.update(shared)
            in_maps.append({k: v for k, v in m.items() if k in nc._in_names})
        res = run_bass_kernel_spmd(nc, in_maps, core_ids=list(range(8)))
        if not final:
            hcur = [np.asarray(r["h%d" % (layers[-1] + 1)]) for r in res.results]
            hs.append(hcur)
    if want_h:
        return hs, res
    out = np.empty((2, 8192, D), np.float32)
    for c in range(8):
        b, q = c // 4, c % 4
        out[b, 2048 * q:2048 * (q + 1), :] = np.asarray(res.results[c]["outT"]).T
    return out


def kernel(**inp):
    return run_groups(inp, GROUPS)
```

```python
import contextlib
import numpy as np
import concourse.bass as bass
import concourse.mybir as mybir
from concourse.bass_utils import run_bass_kernel_spmd

F32 = mybir.dt.float32
BF16 = mybir.dt.bfloat16
AF = mybir.ActivationFunctionType
ALU = mybir.AluOpType

D = 2048
FF = 5632
NLOC = 2688
NCTX = 256
NALL = NLOC + NCTX
NCH = NALL // 128
OWN0 = 384
EPS = 1e-6
NEG = -1e30
SCALE = 128 ** -0.5
DELTAS = [-6, -4, -2, 0, 2, 4, 6]
SPECIAL = [6, 8, 34, 36]

V_GMIX = 0
V_GFFN = 64
V_ADAB = 128
V_CBDW = 512
V_CLNG = 528
V_CLNB = 544
V_GFIN = 560
V_WDW = 576
V_S = 1072
V_VALIDB = 1104
V_CONVV = 1152
V_ALNG = 1154
V_ALNB = 1186
V_IDENT = 1218
NV = 1346


class Buf:
    __slots__ = ("name", "last_w", "readers", "semkey")

    def __init__(self, name):
        self.name = name
        self.last_w = None
        self.readers = []
        self.semkey = None


class Op:
    __slots__ = ("eng", "semkey", "count", "is_dma")

    def __init__(self, eng, semkey, count, is_dma):
        self.eng = eng
        self.semkey = semkey
        self.count = count
        self.is_dma = is_dma


ENGS = ("pe", "act", "dve", "pool", "sp")


def call(name, *a, **k):
    return (name, a, k)


class Prog:
    def __init__(self, nc, stack):
        self.nc = nc
        self.stack = stack
        self.streams = {e: [] for e in ENGS}
        self.sems = {}
        self.sem_total = {}
        self.seen = {e: {} for e in ENGS}
        for e in ENGS:
            self._sem("eng_" + e)
        self.n_dma_sems = 0

    def _sem(self, key):
        if key not in self.sems:
            self.sems[key] = self.stack.enter_context(self.nc.semaphore(key))
            self.sem_total[key] = 0
        return self.sems[key]

    def _need(self, eng, dep, acc):
        if dep is None:
            return
        if dep.is_dma:
            val = self.sem_total[dep.semkey]
        else:
            if dep.eng == "pe" and eng == "pe":
                return
            val = dep.count
        if acc.get(dep.semkey, 0) < val:
            acc[dep.semkey] = val

    def _flush(self, eng, acc):
        for key, val in acc.items():
            if self.seen[eng].get(key, 0) >= val:
                continue
            self.seen[eng][key] = val
            self.streams[eng].append(("wait", self.sems[key], val))

    def _deps(self, eng, reads, writes):
        acc = {}
        for b in reads:
            self._need(eng, b.last_w, acc)
        for b in writes:
            self._need(eng, b.last_w, acc)
            for r in b.readers:
                self._need(eng, r, acc)
        self._flush(eng, acc)

    def _commit(self, op, reads, writes):
        for b in reads:
            b.readers.append(op)
        for b in writes:
            b.last_w = op
            b.readers = []

    def op(self, eng, fn, reads=(), writes=()):
        self._deps(eng, reads, writes)
        key = "eng_" + eng
        self.sem_total[key] += 1
        o = Op(eng, key, self.sem_total[key], False)
        self.streams[eng].append(("op", fn, self.sems[key], 1))
        self._commit(o, reads, writes)
        return o

    def dma(self, queue, out, in_, reads, writes, sembuf):
        self._deps(queue, reads, writes)
        if sembuf.semkey is None:
            sembuf.semkey = "dma_%d" % self.n_dma_sems
            self.n_dma_sems += 1
            self._sem(sembuf.semkey)
        key = sembuf.semkey
        self.sem_total[key] += 16
        o = Op(queue, key, self.sem_total[key], True)
        self.streams[queue].append(("op", call('dma_start', out=out, in_=in_), self.sems[key], 16))
        self._commit(o, reads, writes)
        return o

    def final_wait(self, eng, bufs):
        acc = {}
        for b in bufs:
            self._need(eng, b.last_w, acc)
        self._flush(eng, acc)

    def emit(self, block):
        def run(name):
            def body(e):
                for it in self.streams[name]:
                    if it[0] == "wait":
                        e.wait_ge(it[1], it[2])
                    else:
                        f = it[1]
                        ins = getattr(e, f[0])(*f[1], **f[2]) if isinstance(f, tuple) else f(e)
                        ins.then_inc(it[2], it[3])
            return body
        block.sync(run("sp"))
        block.tensor(run("pe"))
        block.scalar(run("act"))
        block.vector(run("dve"))
        block.gpsimd(run("pool"))


class Builder:
    def __init__(self, layers, final):
        self.layers = layers
        self.final = final
        self.nc = bass.Bass("TRN2", target_bir_lowering=False)
        self.stack = contextlib.ExitStack()
        self.in_names = set()
        self.scrb = Buf("scr")
        self.smallb = Buf("small")

    def dram(self, name, shape, dt, kind):
        if kind == "ExternalInput":
            self.in_names.add(name)
        return self.nc.dram_tensor(name, list(shape), dt, kind=kind).ap()

    def sb(self, name, shape, dt):
        return self.stack.enter_context(self.nc.sbuf_tensor(name, list(shape), dt))

    def build(self):
        nc = self.nc
        with self.stack:
            self.P = Prog(nc, self.stack)
            self._declare()
            self._consts()
            self.bg_gen = None
            self.ffn_n = 0
            for l in self.layers:
                for _ in self._ada_gen(l):
                    pass
            for l in self.layers:
                [self._layer0, self._layer1, self._layer2, self._layer3][l](l)
            outs = [self.hb[self.layers[-1] + 1]] if not self.final else [self.outb]
            for bl in outs:
                self.P.final_wait("sp", bl)
            block = self.stack.enter_context(nc.Block())
            self.P.emit(block)
        nc._in_names = self.in_names
        return nc

    def _declare(self):
        L = self.layers
        ext_in = L[0]
        self.h = {}
        self.hb = {}
        for l in range(ext_in, L[-1] + 2):
            if l == 4:
                break
            if l == ext_in:
                kind = "ExternalInput"
            elif l == L[-1] + 1:
                kind = "ExternalOutput"
            else:
                kind = "Internal"
            self.h[l] = self.dram("h%d" % l, [D, NALL], F32, kind)
            self.hb[l] = [Buf("h%d_%d" % (l, c)) for c in range(NCH)]
        if self.final:
            self.out = self.dram("outT", [D, 2048], F32, "ExternalOutput")
            self.outb = [Buf("out_%d" % c) for c in range(16)]
        self.vecs_d = self.dram("vecs", [128, NV], F32, "ExternalInput")
        self.ada_w = {l: self.dram("ada_w%d" % l, [D, 6 * D], F32, "ExternalInput") for l in L}
        self.w1 = {l: self.dram("ffn_w1_%d" % l, [D, FF], F32, "ExternalInput") for l in L}
        self.w3 = {l: self.dram("ffn_w3_%d" % l, [D, FF], F32, "ExternalInput") for l in L}
        self.w2 = {l: self.dram("ffn_w2_%d" % l, [FF, D], F32, "ExternalInput") for l in L}
        self.a_w_in, self.a_w_out, self.a_wsT, self.a_bs = {}, {}, {}, {}
        for l_, ja in ((0, 0), (3, 1)):
            if l_ in L:
                self.a_w_in[ja] = self.dram("a_w_in%d" % ja, [D, 2 * D], F32, "ExternalInput")
                self.a_w_out[ja] = self.dram("a_w_out%d" % ja, [D, D], F32, "ExternalInput")
                self.a_wsT[ja] = self.dram("a_wsT%d" % ja, [128, 16 * 128], F32, "ExternalInput")
                self.a_bs[ja] = self.dram("a_b_s%d" % ja, [D], F32, "ExternalInput")
        if 1 in L:
            self.b_qkv = self.dram("b_w_qkv", [D, 3 * D], F32, "ExternalInput")
            self.b_out = self.dram("b_w_out", [D, D], F32, "ExternalInput")
            self.btab_d = self.dram("btab", [16, 128, 9 * 128], F32, "ExternalInput")
            self.q_d = self.dram("q_scr", [D, NALL], BF16, "Internal")
            self.k_d = self.dram("k_scr", [D, NALL], BF16, "Internal")
            self.v_d = self.dram("v_scr", [NALL, D], BF16, "Internal")
            self.qb = [Buf("q%d" % c) for c in range(NCH)]
            self.kb = [Buf("k%d" % c) for c in range(NCH)]
            self.vb = [Buf("v%d" % c) for c in range(NCH)]
        if 2 in L:
            self.c_pw1 = self.dram("c_w_pw1", [D, 2 * D], F32, "ExternalInput")
            self.c_pw2 = self.dram("c_w_pw2", [D, D], F32, "ExternalInput")
            self.y_d = self.dram("y_scr", [D, NALL], BF16, "Internal")
            self.yb = [Buf("y%d" % c) for c in range(NCH)]

        self.xT = self.sb("xT", [128, 16, 512], F32)
        self.xTb = Buf("xT")
        self.hT = self.sb("hT", [128, 16, 512], BF16)
        self.hTb = [Buf("hT%d" % i) for i in range(16)]
        self.scr = self.sb("scr", [128, 16640], F32)
        self.stg = [self.sb("stg%d" % i, [128, 4096], F32) for i in range(2)]
        self.stgh = [[Buf("stg%d_lo" % i), Buf("stg%d_hi" % i)] for i in range(2)]
        self.wb = [self.sb("wb%d" % i, [128, 4096], BF16) for i in range(2)]
        self.wbb = [Buf("wb%d" % i) for i in range(2)]
        self.ring = [(self.wb[0][:, :], [self.wbb[0]]), (self.wb[1][:, :], [self.wbb[1]])]
        for i in range(2):
            self.ring.append((self.stg[i][:, 0:2048].bitcast(BF16), [self.stgh[i][0]]))
            self.ring.append((self.stg[i][:, 2048:4096].bitcast(BF16), [self.stgh[i][1]]))
        self.ring_i = 0
        self.wcache = {}
        self.stg_i = 0
        self.wb_i = 0
        self.cast_i = 0
        self.vecs = self.sb("vecs_sb", [128, NV], F32)
        self.vecsb = Buf("vecs")
        self.mod = self.sb("mod", [128, 4, 96, 2], F32)
        self.modb = [Buf("mod%d" % l) for l in range(4)]
        self.Amod = self.sb("Amod", [128, 4, 2, 16, 2], F32)
        self.ssb = self.sb("silu_s", [128, 16, 2], F32)
        self.ssbb = Buf("silu_s")
        self.ssb16 = self.sb("silu_s16", [128, 16, 2], BF16)
        self.ones = self.sb("ones", [128, 128], BF16)
        self.onesb = Buf("ones")
        self.small = self.sb("small", [128, 64], F32)
        self.rstd = self.sb("rstd", [128, 512], F32)
        self.rstdb = Buf("rstd")
        self.tmpA = [self.sb("tmpA%d" % i, [128, 512], F32) for i in range(2)]
        self.tmpAb = [Buf("tmpA%d" % i) for i in range(2)]
        self.tmp_i = 0
        self.tmpB = [self.sb("tmpB%d" % i, [128, 512], BF16) for i in range(4)]
        self.tmpBb = [Buf("tmpB%d" % i) for i in range(4)]
        self.tmpB_i = 0
        self.ps = [self.stack.enter_context(nc_ps) for nc_ps in
                   [self.nc.psum_tensor("ps%d" % i, [128, 512], F32) for i in range(8)]]
        self.psb = [Buf("ps%d" % i) for i in range(8)]
        self.ps_i = 0
        self.lps_i = 0
        self.ident = self.sb("ident", [128, 128], BF16)
        self.tmpM = self.sb("tmpM", [128, 512], F32)
        self.tmpMb = Buf("tmpM")
        self.qsb = self.ksb = self.vsb = self.scrb
        self.lr_prev = []
        self.lr_cur = []
        self.LR = self.sb("LR", [128, 6144], F32)
        self.tmpA.append(self.LR[:, 5632:6144])
        self.tmpAb.append(Buf("tmpA2"))
        self.add2 = self.LR[:, 0:2048].rearrange("p (g q) -> p g q", g=16)
        self.wsT = self.LR[:, 2048:3072].bitcast(BF16)
        self.gcb = None

    def lr_begin(self):
        ops = []
        for b in self.lr_cur:
            if b.last_w is not None:
                ops.append(b.last_w)
            ops.extend(b.readers)
        self.lr_prev = ops
        self.lr_cur = []

    def lr_buf(self, name):
        b = Buf(name)
        b.readers = list(self.lr_prev)
        self.lr_cur.append(b)
        return b

    def bank(self):
        i = self.ps_i
        self.ps_i = (self.ps_i + 1) % 5
        return self.ps[i], self.psb[i]

    def lbank(self):
        i = 5 + self.lps_i
        self.lps_i = (self.lps_i + 1) % 3
        return self.ps[i], self.psb[i]

    def tmpa(self):
        i = self.tmp_i
        self.tmp_i = (i + 1) % 3
        return self.tmpA[i], self.tmpAb[i]

    def tmpb(self):
        i = self.tmpB_i
        self.tmpB_i = (i + 1) % 4
        return self.tmpB[i], self.tmpBb[i]

    def vcol(self, off, n=1):
        return self.vecs[:, off:off + n]

    def _consts(self):
        P = self.P
        P.dma("sp", self.vecs[:, :], self.vecs_d, [], [self.vecsb], self.vecsb)
        P.op("dve", call('memset', self.ones[:, :], 1.0), [], [self.onesb])
        P.op("dve", call('memset', self.small[:, 0:1], EPS), [], [self.onesb])
        P.op("dve", call('tensor_copy', out=self.ident[:, :], in_=self.vecs[:, V_IDENT:V_IDENT + 128]),
             [self.vecsb], [self.onesb])
        P.op("act", call('activation',
            out=self.ssb[:, :, :], in_=self.vecs[:, V_S:V_S + 32].rearrange("p (c k) -> p c k", k=2),
            func=AF.Silu), [self.vecsb], [self.ssbb])
        P.op("dve", call('tensor_copy', out=self.ssb16[:, :, :], in_=self.ssb[:, :, :]), [self.ssbb], [self.ssbb])

    def load_w(self, W, kc0, nk, c0, ncols, cast=True, cache=True):
        P = self.P
        n = nk * ncols
        ent = None
        if cast and cache:
            name = W.name
            ent = self.wcache.get(name)
            if ent is None:
                K_, N_ = W.shape
                ntiles = (K_ * N_) // (128 * n)
                ent = self.wcache[name] = {"ap": self.dram(name + "_bf", [ntiles, 128, n], BF16, "Internal"), "idx": {}}
            key = (kc0, nk, c0, ncols)
            if key in ent["idx"]:
                idx, cbuf = ent["idx"][key]
                ri = self.ring_i
                self.ring_i = (ri + 1) % len(self.ring)
                dst, bl = self.ring[ri]
                P.dma("sp", dst[:, 0:n], ent["ap"][idx], [cbuf], bl, bl[0])
                return dst[:, 0:n].rearrange("p (k n) -> p k n", k=nk), bl
        si = self.stg_i
        self.stg_i ^= 1
        bl = self.stgh[si]
        stg_ap = self.stg[si][:, 0:n].rearrange("p (k n) -> p k n", k=nk)
        src = W[kc0 * 128:(kc0 + nk) * 128, c0:c0 + ncols].rearrange("(k p) n -> p k n", p=128)
        P.dma("sp", stg_ap, src, [], bl, bl[0])
        if not cast:
            return stg_ap, bl
        wi = self.wb_i
        self.wb_i = (self.wb_i + 1) % 2
        eng = ("act", "dve")[self.cast_i % 2]
        self.cast_i += 1
        dst = self.wb[wi][:, 0:n]
        srcf = self.stg[si][:, 0:n]
        if eng == "act":
            P.op("act", call('activation', out=dst, in_=srcf, func=AF.Copy), bl, [self.wbb[wi]])
        else:
            P.op(eng, call('tensor_copy', out=dst, in_=srcf), bl, [self.wbb[wi]])
        if ent is None:
            return dst.rearrange("p (k n) -> p k n", k=nk), [self.wbb[wi]]
        idx = len(ent["idx"])
        cbuf = Buf("wc")
        ent["idx"][key] = (idx, cbuf)
        P.dma("pool" if eng == "pool" else "act", ent["ap"][idx], dst, [self.wbb[wi]], [cbuf], self.wbb[wi])
        return dst.rearrange("p (k n) -> p k n", k=nk), [self.wbb[wi]]

    def bg_step(self):
        if self.bg_gen is None:
            return
        try:
            next(self.bg_gen)
        except StopIteration:
            self.bg_gen = None

    def _ada_gen(self, l):
        P = self.P
        W = self.ada_w[l]
        for nb in range(24):
            if nb:
                yield
            ps, psb = self.bank()
            for kg in range(2):
                wt, wtb = self.load_w(W, kg * 8, 8, nb * 512, 512, cast=True, cache=False)
                for kc in range(8):
                    kk = kg * 8 + kc
                    P.op("pe", call('matmul', ps[0:2, :], self.ssb16[:, kk, :], wt[:, kc, :],
                                    start=(kk == 0), stop=(kk == 15)), wtb + [self.ssbb], [psb])
            t, tb = self.tmpa()
            P.op("dve", call('tensor_copy', out=t[0:2, :], in_=ps[0:2, :]), [psb], [tb])
            pt, ptb = self.bank()
            for i in range(4):
                P.op("pe", call('transpose', pt[:, 2 * i:2 * i + 2], t[0:2, i * 128:(i + 1) * 128],
                                self.vecs[0:2, V_IDENT:V_IDENT + 2]), [tb, self.vecsb], [ptb])
            a0 = V_ADAB + l * 96 + nb * 4
            P.op("dve", call('tensor_tensor', out=self.mod[:, l, nb * 4:nb * 4 + 4, :],
                             in0=pt[:, 0:8].rearrange("p (n k) -> p n k", k=2),
                             in1=self.vecs[:, a0:a0 + 4].unsqueeze(2).broadcast_to([128, 4, 2]), op=ALU.add),
                 [ptb, self.vecsb], [self.modb[l]])
        for which, (goff, sc0) in enumerate(((V_GMIX, 16), (V_GFFN, 64))):
            for k in range(2):
                P.op("dve", call('scalar_tensor_tensor',
                    out=self.Amod[:, l, which, :, k], in0=self.mod[:, l, sc0:sc0 + 16, k], scalar=1.0,
                    in1=self.vecs[:, goff + l * 16:goff + l * 16 + 16], op0=ALU.add, op1=ALU.mult),
                    [self.modb[l], self.vecsb], [self.modb[l]])

    def sc(self, l, name, dc, kind):
        if name == "A1":
            return self.Amod[:, l, 0, dc, kind:kind + 1]
        if name == "A2":
            return self.Amod[:, l, 1, dc, kind:kind + 1]
        base = {"sh1": 0, "gt1": 32, "sh2": 48, "gt2": 80}[name]
        return self.mod[:, l, base + dc, kind:kind + 1]

    @staticmethod
    def segs_T(segs):
        return sum(s[1] for s in segs)

    @staticmethod
    def seg_chunks(segs):
        out = []
        for c0, n, _ in segs:
            out.extend(range(c0 // 128, (c0 + n) // 128))
        return out

    def load_x(self, l, segs):
        P = self.P
        off = 0
        for c0, n, _ in segs:
            bl = [self.hb[l][c] for c in range(c0 // 128, (c0 + n) // 128)]
            P.dma("act", self.xT[:, :, off:off + n],
                  self.h[l][:, c0:c0 + n].rearrange("(c p) t -> p c t", p=128), bl, [self.xTb], self.xTb)
            off += n

    def store_x(self, l, segs):
        P = self.P
        off = 0
        for c0, n, _ in segs:
            bl = [self.hb[l][c] for c in range(c0 // 128, (c0 + n) // 128)]
            P.dma("act", self.h[l][:, c0:c0 + n].rearrange("(c p) t -> p c t", p=128),
                  self.xT[:, :, off:off + n], [self.xTb], bl, self.xTb)
            off += n

    def rms_stats(self, T):
        P = self.P
        ps, psb = self.bank()
        for dc in range(16):
            sq, sqb = self.tmpb()
            P.op("act", call('activation', out=sq[:, 0:T], in_=self.xT[:, dc, 0:T], func=AF.Square),
                 [self.xTb], [sqb])
            P.op("pe", call('matmul', ps[:, 0:T], self.ones[:, :], sq[:, 0:T],
                                                             start=(dc == 0), stop=(dc == 15)),
                 [sqb, self.onesb], [psb])
        P.op("act", call('activation', out=self.rstd[:, 0:T], in_=ps[:, 0:T], func=AF.Sqrt,
                                                  bias=self.small[:, 0:1], scale=1.0 / D),
             [psb, self.onesb], [self.rstdb])
        P.op("dve", call('reciprocal', out=self.rstd[:, 0:T], in_=self.rstd[:, 0:T]), [self.rstdb], [self.rstdb])

    def norm_mod(self, l, segs, An, shn):
        P = self.P
        T = self.segs_T(segs)
        self.rms_stats(T)
        for dc in range(16):
            off = 0
            for c0, n, kind in segs:
                t, tb = self.tmpa()
                P.op("dve", call('scalar_tensor_tensor',
                    out=t[:, 0:n], in0=self.xT[:, dc, off:off + n], scalar=self.sc(l, An, dc, kind),
                    in1=self.rstd[:, off:off + n], op0=ALU.mult, op1=ALU.mult),
                    [self.xTb, self.rstdb, self.modb[l]], [tb])
                P.op("act", call('activation',
                    out=self.hT[:, dc, off:off + n], in_=t[:, 0:n], func=AF.Identity,
                    bias=self.sc(l, shn, dc, kind), scale=1.0),
                    [tb, self.modb[l]], [self.hTb[dc]])
                off += n

    def resid_add(self, l, segs, gtn, dc, ps, psb):
        P = self.P
        off = 0
        for c0, n, kind in segs:
            P.op("dve", call('scalar_tensor_tensor',
                out=self.xT[:, dc, off:off + n], in0=ps[:, off:off + n], scalar=self.sc(l, gtn, dc, kind),
                in1=self.xT[:, dc, off:off + n], op0=ALU.mult, op1=ALU.add),
                [psb, self.xTb, self.modb[l]], [self.xTb])
            off += n

    def linear_ws(self, W, c0, ncols, rhs, rhsb, T, evac, KC=16):
        P = self.P
        for blk in range(ncols // 256):
            if KC == 16:
                wts = [self.load_w(W, 0, 16, c0 + blk * 256, 256)]
                per = 16
            else:
                wts = None
                per = 11
            banks = [self.bank() for _ in range(2)]
            ng = KC // per
            for kg in range(ng):
                if wts is not None:
                    wt, wtb = wts[0]
                else:
                    wt, wtb = self.load_w(W, kg * per, per, c0 + blk * 256, 256)
                for j in range(2):
                    ps, psb = banks[j]
                    for kc in range(per):
                        kk = kg * per + kc
                        P.op("pe", call('matmul',
                            ps[:, 0:T], wt[:, kc, j * 128:(j + 1) * 128], rhs[:, kk, 0:T],
                            start=(kk == 0), stop=(kk == KC - 1)),
                            wtb + [rhsb[kk] if isinstance(rhsb, list) else rhsb], [psb])
            for j in range(2):
                evac(blk * 2 + j, banks[j][0], banks[j][1])

    def ffn(self, l, segs):
        P = self.P
        T = self.segs_T(segs)
        gT = self.scr[:, 0:11264].bitcast(BF16).rearrange("p (c t) -> p c t", c=44)
        gTb = self.scrb
        self.norm_mod(l, segs, "A2", "sh2")
        W1, W3, W2 = self.w1[l], self.w3[l], self.w2[l]
        for blk in range(22):
            w1t, w1b = self.load_w(W1, 0, 16, blk * 256, 256)
            w3t, w3b = self.load_w(W3, 0, 16, blk * 256, 256)
            for j in range(2):
                fc = blk * 2 + j
                pa, pab = self.bank()
                pb, pbb = self.bank()
                for kc in range(16):
                    P.op("pe", call('matmul',
                        pa[:, 0:T], w1t[:, kc, j * 128:(j + 1) * 128], self.hT[:, kc, 0:T],
                        start=(kc == 0), stop=(kc == 15)), w1b + [self.hTb[kc]], [pab])
                for kc in range(16):
                    P.op("pe", call('matmul',
                        pb[:, 0:T], w3t[:, kc, j * 128:(j + 1) * 128], self.hT[:, kc, 0:T],
                        start=(kc == 0), stop=(kc == 15)), w3b + [self.hTb[kc]], [pbb])
                t, tb = self.tmpa()
                P.op("act", call('activation', out=t[:, 0:T], in_=pa[:, 0:T], func=AF.Silu), [pab], [tb])
                P.op("dve", call('tensor_tensor',
                    out=gT[:, fc, 0:T], in0=t[:, 0:T], in1=pb[:, 0:T], op=ALU.mult), [tb, pbb], [gTb])
            if self.ffn_n > 0:
                self.bg_step()
        self.ffn_n += 1
        self.linear_ws(W2, 0, D, gT, gTb, T,
                       lambda j, ps, psb: self.resid_add(l, segs, "gt2", j, ps, psb), KC=44)

    def gmlp_consts(self, ja):
        P = self.P
        self.lr_begin()
        self.gcb = self.lr_buf("gmlp_consts")
        si = self.stg_i
        self.stg_i ^= 1
        P.dma("sp", self.stg[si][:, 0:D], self.a_wsT[ja], [], self.stgh[si], self.stgh[si][0])
        P.op("dve", call('tensor_copy', out=self.wsT[:, :], in_=self.stg[si][:, 0:D]), self.stgh[si], [self.gcb])
        s2 = self.stg_i
        self.stg_i ^= 1
        P.dma("sp", self.stg[s2][:, 0:D], self.a_bs[ja].partition_broadcast(128), [], self.stgh[s2], self.stgh[s2][0])
        for gq in range(4):
            ps, psb = self.bank()
            P.op("pe", call('matmul', ps[:, :], self.ones[:, :], self.wsT[:, gq * 512:(gq + 1) * 512],
                                                       start=True, stop=True), [self.gcb, self.onesb], [psb])
            for gi in range(4):
                g = gq * 4 + gi
                P.op("dve", call('scalar_tensor_tensor',
                    out=self.add2[:, g, :], in0=ps[:, gi * 128:(gi + 1) * 128], scalar=self.vcol(V_ALNB + ja * 16 + g),
                    in1=self.stg[s2][:, g * 128:(g + 1) * 128], op0=ALU.mult, op1=ALU.add),
                    [psb, self.vecsb] + self.stgh[s2], [self.gcb])

    def gmlp(self, l, ja, segs):
        P = self.P
        T = self.segs_T(segs)
        ns = T // 128
        scrb = self.scrb
        vtok = self.scr[:, 0:8192].rearrange("p (s d) -> p s d", s=4)
        uT = self.scr[:, 8192:12288].bitcast(BF16).rearrange("p (c t) -> p c t", c=16)
        vln = self.scr[:, 12288:16384].bitcast(BF16).rearrange("p (s d) -> p s d", s=4)
        Win = self.a_w_in[ja]
        self.norm_mod(l, segs, "A1", "sh1")
        self.linear_ws(Win, 0, D, self.hT, self.hTb, T,
                       lambda j, ps, psb: P.op("act", call('activation',
                           out=uT[:, j, 0:T], in_=ps[:, 0:T], func=AF.Gelu), [psb], [scrb]))
        for nb in range(4):
            banks = [self.bank() for _ in range(ns)]
            for kg in range(2):
                wt, wtb = self.load_w(Win, kg * 8, 8, D + nb * 512, 512)
                for s in range(ns):
                    ps, psb = banks[s]
                    for kc in range(8):
                        kk = kg * 8 + kc
                        P.op("pe", call('matmul',
                            ps[:, :], self.hT[:, kk, s * 128:(s + 1) * 128], wt[:, kc, :],
                            start=(kk == 0), stop=(kk == 15)), wtb + [self.hTb[kk]], [psb])
            for s in range(ns):
                ps, psb = banks[s]
                P.op("act", call('activation',
                    out=vtok[:, s, nb * 512:(nb + 1) * 512], in_=ps[:, :], func=AF.Gelu), [psb], [scrb])
        st = self.small
        for s in range(ns):
            for q in range(4):
                P.op("dve", call('bn_stats', out=st[:, 8 + q * 6:14 + q * 6], in_=vtok[:, s, q * 512:(q + 1) * 512]),
                     [scrb], [self.smallb])
            P.op("dve", call('bn_aggr', out=st[:, 2:4], in_=st[:, 8:32]), [self.smallb], [self.smallb])
            P.op("act", call('activation', out=st[:, 4:5], in_=st[:, 3:4], func=AF.Sqrt, bias=st[:, 0:1], scale=1.0),
                 [self.smallb, self.onesb], [self.smallb])
            P.op("dve", call('reciprocal', out=st[:, 5:6], in_=st[:, 4:5]), [self.smallb], [self.smallb])
            P.op("dve", call('tensor_scalar', out=vln[:, s, :], in0=vtok[:, s, :], scalar1=st[:, 2:3],
                                                       scalar2=st[:, 5:6], op0=ALU.subtract, op1=ALU.mult),
                 [scrb, self.smallb], [scrb])
        for s in range(ns):
            for gq in range(4):
                ps, psb = self.bank()
                for gi in range(4):
                    g = gq * 4 + gi
                    P.op("pe", call('matmul',
                        ps[:, gi * 128:(gi + 1) * 128], vln[:, s, g * 128:(g + 1) * 128], self.wsT[:, g * 128:(g + 1) * 128],
                        start=True, stop=True), [scrb, self.gcb], [psb])
                t, tb = self.tmpa()
                for gi in range(4):
                    g = gq * 4 + gi
                    P.op("dve", call('scalar_tensor_tensor',
                        out=t[:, gi * 128:(gi + 1) * 128], in0=ps[:, gi * 128:(gi + 1) * 128],
                        scalar=self.vcol(V_ALNG + ja * 16 + g), in1=self.add2[:, g, :], op0=ALU.mult, op1=ALU.add),
                        [psb, self.gcb, self.vecsb], [tb])
                P.op("dve", call('tensor_tensor',
                    out=self.hT[:, gq * 4:(gq + 1) * 4, s * 128:(s + 1) * 128],
                    in0=t[:, :].rearrange("p (g q) -> p g q", g=4),
                    in1=uT[:, gq * 4:(gq + 1) * 4, s * 128:(s + 1) * 128], op=ALU.mult),
                    [tb, scrb], self.hTb[gq * 4:gq * 4 + 4])
        self.linear_ws(self.a_w_out[ja], 0, D, self.hT, self.hTb, T,
                       lambda j, ps, psb: self.resid_add(l, segs, "gt1", j, ps, psb))

    def _layer0(self, l):
        self._gmlp_layer(l, 0, [[(0, 384, 0)], [(384, 512, 0)], [(896, 512, 0)], [(1408, 512, 0)], [(1920, 512, 0)],
                                [(2432, 256, 0), (2688, 256, 1)]])

    def _layer3(self, l):
        self._gmlp_layer(l, 1, [[(384 + 512 * i, 512, 0)] for i in range(4)])

    def _gmlp_layer(self, l, ja, tiles):
        self.gmlp_consts(ja)
        for segs in tiles:
            self.load_x(l, segs)
            self.gmlp(l, ja, segs)
            self.ffn(l, segs)
            if l == 3 and self.final:
                self.final_norm(segs)
            else:
                self.store_x(l + 1, segs)

    def final_norm(self, segs):
        P = self.P
        T = self.segs_T(segs)
        self.rms_stats(T)
        for dc in range(16):
            P.op("dve", call('scalar_tensor_tensor',
                out=self.xT[:, dc, 0:T], in0=self.xT[:, dc, 0:T], scalar=self.vcol(V_GFIN + dc),
                in1=self.rstd[:, 0:T], op0=ALU.mult, op1=ALU.mult), [self.xTb, self.rstdb, self.vecsb], [self.xTb])
        c0 = segs[0][0] - OWN0
        bl = [self.outb[c] for c in range(c0 // 128, (c0 + T) // 128)]
        P.dma("act", self.out[:, c0:c0 + T].rearrange("(c p) t -> p c t", p=128), self.xT[:, :, 0:T],
              [self.xTb], bl, self.xTb)

    @staticmethod
    def pair_cks(lr0):
        if lr0 == 6:
            return list(range(1, 7))
        if lr0 == 36:
            return list(range(15, 21))
        return [ck for ck in range((lr0 - 4) // 2, (lr0 + 4) // 2 + 1) if ck <= 20]

    def _layer1(self, l):
        P = self.P
        Wq = self.b_qkv
        qs = self.scr[:, 0:4096].bitcast(BF16).rearrange("p (c t) -> p c t", c=16)
        ks = self.scr[:, 4096:8192].bitcast(BF16).rearrange("p (c t) -> p c t", c=16)
        vs = self.scr[:, 8192:12288].bitcast(BF16).rearrange("p (s d) -> p s d", s=4)
        tiles = [[(512 * i, 512, 0)] for i in range(5)] + [[(2560, 128, 0), (2688, 256, 1)]]
        for segs in tiles:
            T = self.segs_T(segs)
            ns = T // 128
            self.load_x(l, segs)
            self.norm_mod(l, segs, "A1", "sh1")
            self.linear_ws(Wq, 0, D, self.hT, self.hTb, T,
                           lambda j, ps, psb: P.op("act", call('activation',
                               out=qs[:, j, 0:T], in_=ps[:, 0:T], func=AF.Copy), [psb], [self.qsb]))
            self.linear_ws(Wq, D, D, self.hT, self.hTb, T,
                           lambda j, ps, psb: P.op("dve", call('tensor_copy',
                               out=ks[:, j, 0:T], in_=ps[:, 0:T]), [psb], [self.ksb]))
            for nb in range(4):
                banks = [self.bank() for _ in range(ns)]
                for kg in range(2):
                    wt, wtb = self.load_w(Wq, kg * 8, 8, 2 * D + nb * 512, 512)
                    for s in range(ns):
                        ps, psb = banks[s]
                        for kc in range(8):
                            kk = kg * 8 + kc
                            P.op("pe", call('matmul',
                                ps[:, :], self.hT[:, kk, s * 128:(s + 1) * 128], wt[:, kc, :],
                                start=(kk == 0), stop=(kk == 15)), wtb + [self.hTb[kk]], [psb])
                for s in range(ns):
                    ps, psb = banks[s]
                    P.op("act", call('activation',
                        out=vs[:, s, nb * 512:(nb + 1) * 512], in_=ps[:, :], func=AF.Copy), [psb], [self.vsb])
            off = 0
            for c0, n, _ in segs:
                chs = list(range(c0 // 128, (c0 + n) // 128))
                P.dma("act", self.q_d[:, c0:c0 + n].rearrange("(c p) t -> p c t", p=128), qs[:, :, off:off + n],
                      [self.qsb], [self.qb[c] for c in chs], self.qsb)
                P.dma("act", self.k_d[:, c0:c0 + n].rearrange("(c p) t -> p c t", p=128), ks[:, :, off:off + n],
                      [self.ksb], [self.kb[c] for c in chs], self.ksb)
                P.dma("act", self.v_d[c0:c0 + n, :].rearrange("(s p) d -> p s d", p=128),
                      vs[:, off // 128:(off + n) // 128, :], [self.vsb], [self.vb[c] for c in chs], self.vsb)
                off += n
        LRb = self.LR[:, :].bitcast(BF16)
        self.lr_begin()
        kh, vh, qh, bt, slotb = [], [], [], [], []
        for sl in range(2):
            base = sl * 2688
            kh.append(LRb[:, 2 * base:2 * base + 1280])
            vh.append(LRb[:, 2 * base + 1280:2 * base + 2560].rearrange("p (c d) -> p c d", c=10))
            qh.append(LRb[:, 2 * base + 2560:2 * base + 3072])
            bt.append(self.LR[:, base + 1536:base + 2688].rearrange("p (s q) -> p s q", s=9))
            slotb.append(self.lr_buf("aslot%d" % sl))
        pT = [LRb[:, 10752 + i * 128:10752 + (i + 1) * 128] for i in range(4)]
        pTb = [self.lr_buf("pT%d" % i) for i in range(4)]
        pti = [0]
        vbv = self.vecs[:, V_VALIDB:V_VALIDB + 48].rearrange("p (s c r) -> p s c r", s=4, c=6)
        tiles = [[(256 + 512 * i, 512, 0)] for i in range(4)] + [[(2304, 256, 0), (2688, 256, 1)]]

        att_t = [self.scr[:, 12288 + i * 128:12288 + (i + 1) * 128] for i in range(8)]
        att_tb = [Buf("att_t%d" % i) for i in range(8)]
        att_pv = self.scr[:, 13312:13824].bitcast(BF16)
        att_p = [att_pv[:, i * 128:(i + 1) * 128] for i in range(8)]
        att_pb = [Buf("att_p%d" % i) for i in range(8)]

        def attend(h, sl, qcol0, blocks, lr0, ck_lo, ocol0):
            o_ps, o_psb = self.lbank()
            d_ps, d_psb = self.lbank()
            nb_ = len(blocks)
            qcols = qh[sl][:, qcol0:qcol0 + 128]
            s_aps, s_bufs, vblks = [], [], []
            for bi, (ck, kind) in enumerate(blocks):
                if bi % 4 == 0:
                    s_ps, s_psb = self.bank()
                if kind == 0:
                    kcols = kh[sl][:, (ck - ck_lo) * 128:(ck - ck_lo + 1) * 128]
                    vblks.append(vh[sl][:, ck - ck_lo, :])
                else:
                    kcols = kh[sl][:, 1024 + ck * 128:1024 + (ck + 1) * 128]
                    vblks.append(vh[sl][:, 8 + ck, :])
                s_ap = s_ps[:, (bi % 4) * 128:(bi % 4 + 1) * 128]
                s_aps.append(s_ap)
                s_bufs.append(s_psb)
                P.op("pe", call('matmul', s_ap, kcols, qcols, start=True, stop=True), [slotb[sl]], [s_psb])
            for bi, (ck, kind) in enumerate(blocks):
                s_ap, s_psb = s_aps[bi], s_bufs[bi]
                p_ap, p_b = att_p[bi], att_pb[bi]
                if kind == 0:
                    delta = 2 * ck - lr0
                    special = lr0 in SPECIAL
                    if special or delta not in (-4, 4):
                        idx = DELTAS.index(delta)
                    else:
                        idx = 7 if delta == -4 else 8
                    t, tb = att_t[bi], att_tb[bi]
                    P.op("dve", call('scalar_tensor_tensor', out=t, in0=s_ap, scalar=SCALE, in1=bt[sl][:, idx, :],
                                     op0=ALU.mult, op1=ALU.add), [s_psb, slotb[sl]], [tb])
                    if special:
                        si = SPECIAL.index(lr0)
                        ci = self.pair_cks(lr0).index(ck)
                        for rq in range(2):
                            P.op("act", call('activation', out=p_ap[:, rq * 64:(rq + 1) * 64],
                                             in_=t[:, rq * 64:(rq + 1) * 64], func=AF.Exp,
                                             bias=vbv[:, si, ci, rq:rq + 1], scale=1.0), [tb, self.vecsb], [p_b])
                    else:
                        P.op("act", call('activation', out=p_ap, in_=t, func=AF.Exp), [tb], [p_b])
                else:
                    P.op("act", call('activation', out=p_ap, in_=s_ap, func=AF.Exp, scale=SCALE), [s_psb], [p_b])
            for bi in range(nb_):
                p_ap, p_b = att_p[bi], att_pb[bi]
                P.op("pe", call('matmul', o_ps[:, 0:128], vblks[bi], p_ap, start=(bi == 0), stop=(bi == nb_ - 1)),
                     [slotb[sl], p_b], [o_psb])
                P.op("pe", call('matmul', d_ps[:, 0:128], self.ones[:, :], p_ap, start=(bi == 0), stop=(bi == nb_ - 1)),
                     [self.onesb, p_b], [d_psb])
            t, tb = self.tmpa()
            P.op("dve", call('reciprocal', out=t[:, 0:128], in_=d_ps[:, 0:128]), [d_psb], [tb])
            P.op("dve", call('tensor_tensor', out=self.hT[:, h, ocol0:ocol0 + 128], in0=o_ps[:, 0:128], in1=t[:, 0:128],
                             op=ALU.mult), [o_psb, tb], [self.hTb[h]])

        for segs in tiles:
            T = self.segs_T(segs)
            self.load_x(l, segs)
            lat = segs[0]
            npairs = lat[1] // 128
            pairs = [lat[0] // 64 + 2 * i for i in range(npairs)]
            allck = [ck for p_ in pairs for ck in self.pair_cks(p_)]
            ck_lo, ck_hi = min(allck), max(allck)
            nck = ck_hi - ck_lo + 1
            assert nck <= 8
            has_ctx = len(segs) > 1
            for h in range(16):
                sl = h % 2
                hr = slice(h * 128, (h + 1) * 128)
                kbl = [self.kb[c] for c in range(ck_lo, ck_hi + 1)] + [self.kb[21], self.kb[22]]
                vbl = [self.vb[c] for c in range(ck_lo, ck_hi + 1)] + [self.vb[21], self.vb[22]]
                P.dma("sp", kh[sl][:, 0:nck * 128], self.k_d[hr, ck_lo * 128:(ck_hi + 1) * 128], kbl, [slotb[sl]], slotb[sl])
                P.dma("sp", kh[sl][:, 1024:1280], self.k_d[hr, NLOC:NALL], kbl, [slotb[sl]], slotb[sl])
                P.dma("sp", vh[sl][:, 0:nck, :], self.v_d[ck_lo * 128:(ck_hi + 1) * 128, hr].rearrange("(c p) d -> p c d", p=128),
                      vbl, [slotb[sl]], slotb[sl])
                P.dma("sp", vh[sl][:, 8:10, :], self.v_d[NLOC:NALL, hr].rearrange("(c p) d -> p c d", p=128),
                      vbl, [slotb[sl]], slotb[sl])
                off = 0
                for c0, n, _ in segs:
                    P.dma("sp", qh[sl][:, off:off + n], self.q_d[hr, c0:c0 + n],
                          [self.qb[c] for c in range(c0 // 128, (c0 + n) // 128)], [slotb[sl]], slotb[sl])
                    off += n
                P.dma("sp", bt[sl], self.btab_d[h].rearrange("p (s q) -> p s q", s=9), [], [slotb[sl]], slotb[sl])
                for pi, lr0 in enumerate(pairs):
                    blocks = [(ck, 0) for ck in self.pair_cks(lr0)] + [(0, 1), (1, 1)]
                    attend(h, sl, pi * 128, blocks, lr0, ck_lo, pi * 128)
                if has_ctx:
                    for cp in range(2):
                        attend(h, sl, lat[1] + cp * 128, [(0, 1), (1, 1)], None, ck_lo, lat[1] + cp * 128)
            self.linear_ws(self.b_out, 0, D, self.hT, self.hTb, T,
                           lambda j, ps, psb: self.resid_add(l, segs, "gt1", j, ps, psb))
            self.ffn(l, segs)
            self.store_x(l + 1, segs)

    def _layer2(self, l):
        P = self.P
        ys = self.scr[:, 0:4096].bitcast(BF16).rearrange("p (c t) -> p c t", c=16)
        ysb = self.scrb
        W1 = self.c_pw1
        tiles = [[(256 + 512 * i, 512, 0)] for i in range(4)] + [[(2304, 256, 0), (2688, 256, 1)]]
        for segs in tiles:
            T = self.segs_T(segs)
            self.load_x(l, segs)
            self.norm_mod(l, segs, "A1", "sh1")
            for blk in range(8):
                wa, wab = self.load_w(W1, 0, 16, blk * 256, 256)
                wg, wgb = self.load_w(W1, 0, 16, D + blk * 256, 256)
                for j in range(2):
                    pa, pab = self.bank()
                    pg, pgb = self.bank()
                    for kc in range(16):
                        P.op("pe", call('matmul',
                            pa[:, 0:T], wa[:, kc, j * 128:(j + 1) * 128], self.hT[:, kc, 0:T],
                            start=(kc == 0), stop=(kc == 15)), wab + [self.hTb[kc]], [pab])
                    for kc in range(16):
                        P.op("pe", call('matmul',
                            pg[:, 0:T], wg[:, kc, j * 128:(j + 1) * 128], self.hT[:, kc, 0:T],
                            start=(kc == 0), stop=(kc == 15)), wgb + [self.hTb[kc]], [pgb])
                    t, tb = self.tmpa()
                    P.op("act", call('activation', out=t[:, 0:T], in_=pg[:, 0:T], func=AF.Sigmoid), [pgb], [tb])
                    P.op("dve", call('tensor_tensor',
                        out=ys[:, blk * 2 + j, 0:T], in0=pa[:, 0:T], in1=t[:, 0:T], op=ALU.mult), [pab, tb], [ysb])
            off = 0
            for c0, n, _ in segs:
                P.dma("act", self.y_d[:, c0:c0 + n].rearrange("(c p) t -> p c t", p=128), ys[:, :, off:off + n],
                      [ysb], [self.yb[c] for c in range(c0 // 128, (c0 + n) // 128)], ysb)
                off += n
        ywin = self.scr[:, 4096:8448].bitcast(BF16).rearrange("p (c t) -> p c t", c=16)
        ywb = self.scrb
        z = self.scr[:, 8448:16640].rearrange("p (c t) -> p c t", c=16)
        zb_ = self.scrb
        LRb = self.LR[:, :].bitcast(BF16)
        wdiag = [LRb[:, i * 3968:(i + 1) * 3968].rearrange("p (j m) -> p j m", j=31) for i in range(2)]
        self.lr_begin()
        wdb = [self.lr_buf("wdiag%d" % i) for i in range(2)]
        wdw = self.vecs[:, V_WDW:V_WDW + 496].rearrange("p (c j) -> p c j", c=16)
        tiles = [[(384 + 512 * i, 512, 0)] for i in range(4)] + [[(2688, 256, 1)]]
        for segs in tiles:
            c0, T, kind = segs[0]
            self.load_x(l, segs)
            if kind == 0:
                P.dma("sp", ywin[:, :, 0:T + 30], self.y_d[:, c0 - 15:c0 + T + 15].rearrange("(c p) t -> p c t", p=128),
                      [self.yb[c] for c in range((c0 - 15) // 128, (c0 + T + 14) // 128 + 1)], [ywb], ywb)
                if c0 == 384:
                    P.op("dve", call('tensor_scalar', out=ywin[:, :, 0:15], in0=ywin[:, :, 0:15],
                                                          scalar1=self.vcol(V_CONVV), scalar2=None, op0=ALU.mult),
                         [ywb, self.vecsb], [ywb])
                if c0 == 1920:
                    P.op("dve", call('tensor_scalar', out=ywin[:, :, T + 15:T + 30], in0=ywin[:, :, T + 15:T + 30],
                                                          scalar1=self.vcol(V_CONVV + 1), scalar2=None, op0=ALU.mult),
                         [ywb, self.vecsb], [ywb])
            else:
                P.op("dve", call('memset', ywin[:, :, 0:15], 0.0), [], [ywb])
                P.op("dve", call('memset', ywin[:, :, T + 15:T + 30], 0.0), [], [ywb])
                P.dma("sp", ywin[:, :, 15:15 + T], self.y_d[:, c0:c0 + T].rearrange("(c p) t -> p c t", p=128),
                      [self.yb[21], self.yb[22]], [ywb], ywb)
            m_ps, m_psb = self.lbank()
            e_ps, e_psb = self.lbank()
            for dc in range(16):
                wi = dc % 2
                P.op("dve", call('tensor_tensor',
                    out=wdiag[wi], in0=self.ident[:, :].unsqueeze(1).broadcast_to([128, 31, 128]),
                    in1=wdw[:, dc, :].unsqueeze(2).broadcast_to([128, 31, 128]), op=ALU.mult),
                    [self.onesb, self.vecsb], [wdb[wi]])
                c_ps, c_psb = self.bank()
                for j in range(31):
                    P.op("pe", call('matmul',
                        c_ps[:, 0:T], wdiag[wi][:, j, :], ywin[:, dc, j:j + T], start=(j == 0), stop=(j == 30)),
                        [wdb[wi], ywb], [c_psb])
                bcol = self.vcol(V_CBDW + dc)
                P.op("act", call('activation',
                    out=z[:, dc, 0:T], in_=c_ps[:, 0:T], func=AF.Identity, bias=bcol, scale=1.0), [c_psb, self.vecsb], [zb_])
                zb16, zb16b = self.tmpb()
                P.op("dve", call('tensor_copy', out=zb16[:, 0:T], in_=z[:, dc, 0:T]), [zb_], [zb16b])
                z2, z2b = self.tmpb()
                P.op("act", call('activation', out=z2[:, 0:T], in_=z[:, dc, 0:T], func=AF.Square), [zb_], [z2b])
                P.op("pe", call('matmul', m_ps[:, 0:T], self.ones[:, :], zb16[:, 0:T],
                                                                start=(dc == 0), stop=(dc == 15)), [zb16b, self.onesb], [m_psb])
                P.op("pe", call('matmul', e_ps[:, 0:T], self.ones[:, :], z2[:, 0:T],
                                                            start=(dc == 0), stop=(dc == 15)), [z2b, self.onesb], [e_psb])
            mt = self.tmpM
            P.op("act", call('activation', out=mt[:, 0:T], in_=m_ps[:, 0:T], func=AF.Copy, scale=1.0 / D), [m_psb], [self.tmpMb])
            t, tb = self.tmpa()
            P.op("dve", call('tensor_tensor', out=t[:, 0:T], in0=mt[:, 0:T], in1=mt[:, 0:T], op=ALU.mult), [self.tmpMb], [tb])
            P.op("dve", call('scalar_tensor_tensor', out=self.rstd[:, 0:T], in0=e_ps[:, 0:T], scalar=1.0 / D,
                                                              in1=t[:, 0:T], op0=ALU.mult, op1=ALU.subtract),
                 [e_psb, tb], [self.rstdb])
            P.op("act", call('activation', out=self.rstd[:, 0:T], in_=self.rstd[:, 0:T], func=AF.Sqrt,
                                               bias=self.small[:, 0:1], scale=1.0), [self.rstdb, self.onesb], [self.rstdb])
            P.op("dve", call('reciprocal', out=self.rstd[:, 0:T], in_=self.rstd[:, 0:T]), [self.rstdb], [self.rstdb])
            for dc in range(16):
                t, tb = self.tmpa()
                P.op("dve", call('tensor_tensor', out=t[:, 0:T], in0=z[:, dc, 0:T], in1=mt[:, 0:T], op=ALU.subtract),
                     [zb_, self.tmpMb], [tb])
                P.op("dve", call('tensor_tensor', out=t[:, 0:T], in0=t[:, 0:T], in1=self.rstd[:, 0:T], op=ALU.mult),
                     [tb, self.rstdb], [tb])
                P.op("act", call('activation',
                    out=self.hT[:, dc, 0:T], in_=t[:, 0:T], func=AF.Silu, bias=self.vcol(V_CLNB + dc),
                    scale=self.vcol(V_CLNG + dc)), [tb, self.vecsb], [self.hTb[dc]])
            self.linear_ws(self.c_pw2, 0, D, self.hT, self.hTb, T,
                           lambda j, ps, psb: self.resid_add(l, segs, "gt1", j, ps, psb))
            self.ffn(l, segs)
            self.store_x(l + 1, segs)


def _fm(v):
    return np.ascontiguousarray(np.asarray(v, np.float32).reshape(16, 128).T)


def _core_inputs(inp, core):
    b, q = core // 4, core % 4
    x = inp["x"][b]
    g0 = 2048 * q - OWN0
    loc = np.zeros((NALL, D), np.float32)
    lo, hi = max(g0, 0), min(g0 + NLOC, 8192)
    loc[lo - g0:hi - g0] = x[lo:hi]
    loc[NLOC:] = inp["ctx"][b]
    h0 = np.ascontiguousarray(loc.T)
    vecs = np.zeros((128, NV), np.float32)
    for l in range(4):
        vecs[:, V_GMIX + 16 * l:V_GMIX + 16 * l + 16] = _fm(inp["g_mix"][l])
        vecs[:, V_GFFN + 16 * l:V_GFFN + 16 * l + 16] = _fm(inp["g_ffn"][l])
        vecs[:, V_ADAB + 96 * l:V_ADAB + 96 * l + 96] = np.asarray(inp["ada_b"][l], np.float32).reshape(96, 128).T
    vecs[:, V_CBDW:V_CBDW + 16] = _fm(inp["c_b_dw"][0])
    vecs[:, V_CLNG:V_CLNG + 16] = _fm(inp["c_ln_g"][0])
    vecs[:, V_CLNB:V_CLNB + 16] = _fm(inp["c_ln_b"][0])
    vecs[:, V_GFIN:V_GFIN + 16] = _fm(inp["g_final"])
    wdw = np.asarray(inp["c_w_dw"][0], np.float32)
    vecs[:, V_WDW:V_WDW + 496] = wdw.reshape(31, 16, 128).transpose(2, 1, 0).reshape(128, 496)
    s = np.stack([_fm(inp["c"][b]), _fm(inp["c_ctx"])], axis=-1)
    vecs[:, V_S:V_S + 32] = s.reshape(128, 32)
    vb = np.zeros((128, 4, 6, 2), np.float32)
    for si, lr0 in enumerate(SPECIAL):
        cks = {6: range(1, 7), 8: range(2, 7), 34: range(15, 20), 36: range(15, 21)}[lr0]
        for ci, ck in enumerate(cks):
            for rq in range(2):
                lr = lr0 + rq
                gr = 32 * q - 6 + lr
                if 0 <= gr < 128:
                    ws = min(max(gr - 4, 0), 120) - (32 * q - 6)
                else:
                    ws = lr - 4
                for rk in range(2):
                    kr = 2 * ck + rk
                    if not (ws <= kr < ws + 8):
                        vb[rk * 64:(rk + 1) * 64, si, ci, rq] = NEG
    vecs[:, V_VALIDB:V_VALIDB + 48] = vb.reshape(128, 48)
    for ja in range(2):
        vecs[:, V_ALNG + 16 * ja:V_ALNG + 16 * ja + 16] = _fm(inp["a_ln_g"][ja])
        vecs[:, V_ALNB + 16 * ja:V_ALNB + 16 * ja + 16] = _fm(inp["a_ln_b"][ja])
    vecs[:, V_IDENT:V_IDENT + 128] = np.eye(128, dtype=np.float32)
    vecs[:, V_CONVV] = 0.0 if q == 0 else 1.0
    vecs[:, V_CONVV + 1] = 0.0 if q == 3 else 1.0
    return h0, vecs


def _btab(rpb):
    rpb = np.asarray(rpb, np.float32)
    cols = np.arange(64)
    cstart = np.clip(cols - 8, 0, 48)
    kc = cols[:, None]
    c = cols[None, :]
    col_in = (kc >= cstart[None, :]) & (kc < cstart[None, :] + 16)
    dc_idx = np.clip(kc - c + 15, 0, 30)
    B = np.where(col_in[None, None], rpb[:, :, dc_idx], np.float32(NEG)).astype(np.float32)
    out = np.full((16, 128, 9, 128), NEG, np.float32)
    slots = [(d, False) for d in DELTAS] + [(-4, True), (4, True)]
    for si, (dl, baked) in enumerate(slots):
        for rk in range(2):
            for rq in range(2):
                dr = dl + rk - rq
                if baked and not (-4 <= dr <= 3):
                    continue
                out[:, rk * 64:(rk + 1) * 64, si, rq * 64:(rq + 1) * 64] = B[:, dr + 7]
    return out.reshape(16, 128, 9 * 128)


_NC_CACHE = {}


def _get_nc(layers, final):
    key = (tuple(layers), final)
    if key not in _NC_CACHE:
        _NC_CACHE[key] = Builder(list(layers), final).build()
    return _NC_CACHE[key]


GROUPS = [[0, 1, 2, 3]]


def shared_inputs(inp):
    sh = {}
    for l in range(4):
        sh["ada_w%d" % l] = inp["ada_w"][l]
        sh["ffn_w1_%d" % l] = inp["ffn_w1"][l]
        sh["ffn_w3_%d" % l] = inp["ffn_w3"][l]
        sh["ffn_w2_%d" % l] = inp["ffn_w2"][l]
    for ja in range(2):
        sh["a_w_in%d" % ja] = inp["a_w_in"][ja]
        sh["a_w_out%d" % ja] = inp["a_w_out"][ja]
        sh["a_wsT%d" % ja] = np.ascontiguousarray(inp["a_w_s"][ja].transpose(2, 0, 1).reshape(128, 2048))
        sh["a_b_s%d" % ja] = np.ascontiguousarray(inp["a_b_s"][ja].reshape(2048))
    sh.update({
        "b_w_qkv": inp["b_w_qkv"][0], "b_w_out": inp["b_w_out"][0], "btab": _btab(inp["b_rpb"][0]),
        "c_w_pw1": inp["c_w_pw1"][0], "c_w_pw2": inp["c_w_pw2"][0],
    })
    return sh


def run_groups(inp, groups, want_h=False, hcur0=None):
    inp = {k: np.asarray(v) for k, v in inp.items()}
    shared = shared_inputs(inp)
    per_core = [_core_inputs(inp, c) for c in range(8)]
    hcur = hcur0 if hcur0 is not None else [pc[0] for pc in per_core]
    res = None
    hs = []
    for layers in groups:
        final = layers[-1] == 3
        nc = _get_nc(layers, final)
        in_maps = []
        for c in range(8):
            m = {"h%d" % layers[0]: hcur[c], "vecs": per_core[c][1]}
            m.update(shared)
            in_maps.append({k: v for k, v in m.items() if k in nc._in_names})
        res = run_bass_kernel_spmd(nc, in_maps, core_ids=list(range(8)))
        if not final:
            hcur = [np.asarray(r["h%d" % (layers[-1] + 1)]) for r in res.results]
            hs.append(hcur)
    if want_h:
        return hs, res
    out = np.empty((2, 8192, D), np.float32)
    for c in range(8):
        b, q = c // 4, c % 4
        out[b, 2048 * q:2048 * (q + 1), :] = np.asarray(res.results[c]["outT"]).T
    return out


def kernel(**inp):
    return run_groups(inp, GROUPS)
```
